# Optimizing a Trainium2 kernel written in Bass

```python
import math
import jax, jax.numpy as jnp
from jax import lax
import numpy as np

D_MODEL = 1024
BATCH = 4
SEQ = 4096
DEPTH = 1

CHUNK = 64
Q_BLOCK = 128
SB_HEADS = 8
SB_HEAD_DIM = 64
DA_HEADS = 4
DA_HEAD_DIM = 64
SB_WIDTH = SB_HEADS * SB_HEAD_DIM
DA_WIDTH = DA_HEADS * 2 * DA_HEAD_DIM
IN_WIDTH = 3 * SB_WIDTH + 3 * DA_WIDTH
N_BRANCH = 2
ROPE_THETA = 500000.0
ROPE_DIM = DA_HEAD_DIM // 4
D_FF = ((8 * D_MODEL // 3 + 255) // 256) * 256
EPS = 1e-6
NEG_INF = -1e30

kernel_name = "hybrid_stickbreak_diffattn_gated_block"


def _rmsnorm(x, g):
    xf = x.astype(jnp.float32)
    y = xf * lax.rsqrt(jnp.mean(xf * xf, axis=-1, keepdims=True) + EPS) * g.astype(jnp.float32)
    return y.astype(x.dtype)


def _heads(t, n_heads):
    b, s, _ = t.shape
    return t.reshape(b, s, n_heads, -1).transpose(0, 2, 1, 3)


def _merge_heads(t):
    b, h, s, d = t.shape
    return t.transpose(0, 2, 1, 3).reshape(b, s, h * d)


def _to_blocks(t):
    b, h, s, d = t.shape
    return t.reshape(b, h, s // Q_BLOCK, Q_BLOCK, d).transpose(2, 0, 1, 3, 4)


def _from_blocks(t):
    nb, b, h, qb, d = t.shape
    return t.transpose(1, 2, 0, 3, 4).reshape(b, h, nb * qb, d)


def _partial_rope(t):
    s = t.shape[2]
    pos = jnp.arange(s, dtype=jnp.float32)
    inv_freq = ROPE_THETA ** (-jnp.arange(0, ROPE_DIM, 2, dtype=jnp.float32) / ROPE_DIM)
    ang = pos[:, None] * inv_freq[None, :]
    cos, sin = jnp.cos(ang), jnp.sin(ang)
    tf = t.astype(jnp.float32)
    half = ROPE_DIM // 2
    x1, x2, rest = tf[..., :half], tf[..., half:ROPE_DIM], tf[..., ROPE_DIM:]
    rot = jnp.concatenate([x1 * cos - x2 * sin, x2 * cos + x1 * sin, rest], axis=-1)
    return rot.astype(t.dtype)


def _stick_breaking_attention(q, k, v):
    s_len = q.shape[2]
    nb = s_len // Q_BLOCK
    scale = q.shape[-1] ** -0.5
    kf = k.astype(jnp.float32)
    vf = v.astype(jnp.float32)
    kpos = jnp.arange(s_len)

    def one_block(args):
        qb, t0 = args
        tpos = t0 + jnp.arange(Q_BLOCK)
        mask = kpos[None, :] < tpos[:, None]
        z = jnp.einsum('bhqd,bhkd->bhqk', qb.astype(jnp.float32), kf) * scale
        log_1mb = jnp.where(mask, jax.nn.log_sigmoid(-z), 0.0)
        between = lax.cumsum(log_1mb, axis=3, reverse=True) - log_1mb
        a = jnp.where(mask, jnp.exp(jax.nn.log_sigmoid(z) + between), 0.0)
        return jnp.einsum('bhqk,bhkd->bhqd', a, vf)

    out = lax.map(one_block, (_to_blocks(q), jnp.arange(nb) * Q_BLOCK))
    return _from_blocks(out).astype(v.dtype)


def _diff_attention(q1, q2, k1, k2, v, lam):
    s_len = q1.shape[2]
    nb = s_len // Q_BLOCK
    scale = q1.shape[-1] ** -0.5
    k1f, k2f, vf = k1.astype(jnp.float32), k2.astype(jnp.float32), v.astype(jnp.float32)
    kchunk = jnp.arange(s_len) // CHUNK
    lamf = lam.astype(jnp.float32)

    def one_block(args):
        q1b, q2b, t0 = args
        tchunk = (t0 + jnp.arange(Q_BLOCK)) // CHUNK
        mask = kchunk[None, :] <= tchunk[:, None]
        s1 = jnp.einsum('bhqd,bhkd->bhqk', q1b.astype(jnp.float32), k1f) * scale
        s2 = jnp.einsum('bhqd,bhkd->bhqk', q2b.astype(jnp.float32), k2f) * scale
        p1 = jax.nn.softmax(jnp.where(mask, s1, NEG_INF), axis=-1)
        p2 = jax.nn.softmax(jnp.where(mask, s2, NEG_INF), axis=-1)
        return jnp.einsum('bhqk,bhkd->bhqd', p1 - lamf * p2, vf)

    out = lax.map(one_block, (_to_blocks(q1), _to_blocks(q2), jnp.arange(nb) * Q_BLOCK))
    return _from_blocks(out).astype(v.dtype)


def setup_inputs(seed: int = 0) -> dict:
    key = jax.random.key(seed)
    ks = jax.random.split(key, 20)
    L = DEPTH

    def w(k, shape, fan_in):
        return jax.random.normal(k, shape, jnp.float32) * fan_in ** -0.5

    def gain(k, shape):
        return 1.0 + 0.02 * jax.random.normal(k, shape, jnp.float32)

    return {
        "x": jax.random.normal(ks[0], (BATCH, SEQ, D_MODEL), jnp.float32),
        "g_mix": gain(ks[1], (L, D_MODEL)),
        "w_in": w(ks[2], (L, D_MODEL, IN_WIDTH), D_MODEL),
        "g_q": gain(ks[3], (L, DA_HEAD_DIM)),
        "g_k": gain(ks[4], (L, DA_HEAD_DIM)),
        "lam_q1": 0.1 * jax.random.normal(ks[5], (L, DA_HEAD_DIM), jnp.float32),
        "lam_k1": 0.1 * jax.random.normal(ks[6], (L, DA_HEAD_DIM), jnp.float32),
        "lam_q2": 0.1 * jax.random.normal(ks[7], (L, DA_HEAD_DIM), jnp.float32),
        "lam_k2": 0.1 * jax.random.normal(ks[8], (L, DA_HEAD_DIM), jnp.float32),
        "g_sub": gain(ks[9], (L, 2 * DA_HEAD_DIM)),
        "w_branch_a": w(ks[10], (L, SB_WIDTH, D_MODEL), SB_WIDTH),
        "w_branch_b": w(ks[11], (L, DA_WIDTH, D_MODEL), DA_WIDTH),
        "w_gate": w(ks[12], (L, D_MODEL, N_BRANCH * D_MODEL), D_MODEL),
        "b_gate": 0.02 * jax.random.normal(ks[13], (L, N_BRANCH * D_MODEL), jnp.float32),
        "w_out": w(ks[14], (L, D_MODEL, D_MODEL), D_MODEL),
        "g_ffn": gain(ks[15], (L, D_MODEL)),
        "w_ffn_gate": w(ks[16], (L, D_MODEL, D_FF), D_MODEL),
        "w_ffn_up": w(ks[17], (L, D_MODEL, D_FF), D_MODEL),
        "w_ffn_down": w(ks[18], (L, D_FF, D_MODEL), D_FF),
    }


def reference(x, g_mix, w_in, g_q, g_k, lam_q1, lam_k1, lam_q2, lam_k2, g_sub,
              w_branch_a, w_branch_b, w_gate, b_gate, w_out, g_ffn,
              w_ffn_gate, w_ffn_up, w_ffn_down):
    b, s, _ = x.shape
    for layer in range(DEPTH):
        lambda_init = 0.8 - 0.6 * math.exp(-0.3 * layer)

        h = _rmsnorm(x, g_mix[layer])
        proj = jnp.einsum('bsd,de->bse', h, w_in[layer])
        sb_q, sb_k, sb_v, da_q, da_k, da_v = jnp.split(
            proj, np.cumsum([SB_WIDTH, SB_WIDTH, SB_WIDTH, DA_WIDTH, DA_WIDTH]).tolist(), axis=-1)

        o_a = _stick_breaking_attention(_heads(sb_q, SB_HEADS), _heads(sb_k, SB_HEADS),
                                        _heads(sb_v, SB_HEADS))
        o_a = _merge_heads(o_a)

        qh = _heads(da_q, DA_HEADS)
        kh = _heads(da_k, DA_HEADS)
        vh = _heads(da_v, DA_HEADS)
        q1 = _partial_rope(_rmsnorm(qh[..., :DA_HEAD_DIM], g_q[layer]))
        q2 = _partial_rope(_rmsnorm(qh[..., DA_HEAD_DIM:], g_q[layer]))
        k1 = _partial_rope(_rmsnorm(kh[..., :DA_HEAD_DIM], g_k[layer]))
        k2 = _partial_rope(_rmsnorm(kh[..., DA_HEAD_DIM:], g_k[layer]))
        lam = (jnp.exp(jnp.sum(lam_q1[layer].astype(jnp.float32) * lam_k1[layer].astype(jnp.float32)))
               - jnp.exp(jnp.sum(lam_q2[layer].astype(jnp.float32) * lam_k2[layer].astype(jnp.float32)))
               + lambda_init)
        o_b = _diff_attention(q1, q2, k1, k2, vh, lam)
        o_b = _rmsnorm(o_b, g_sub[layer]) * (1.0 - lambda_init)
        o_b = _merge_heads(o_b.astype(x.dtype))

        gates = jax.nn.sigmoid(jnp.einsum('bsd,de->bse', h, w_gate[layer]) + b_gate[layer])
        gates = gates.reshape(b, s, N_BRANCH, D_MODEL)
        br_a = jnp.einsum('bsc,cd->bsd', o_a, w_branch_a[layer])
        br_b = jnp.einsum('bsc,cd->bsd', o_b, w_branch_b[layer])
        merged = gates[:, :, 0] * br_a + gates[:, :, 1] * br_b
        x = x + jnp.einsum('bsd,de->bse', merged, w_out[layer])

        h2 = _rmsnorm(x, g_ffn[layer])
        ff = jax.nn.silu(jnp.einsum('bsd,df->bsf', h2, w_ffn_gate[layer])) * \
            jnp.einsum('bsd,df->bsf', h2, w_ffn_up[layer])
        x = x + jnp.einsum('bsf,fd->bsd', ff, w_ffn_down[layer])
    return x
```

```python
import math
from contextlib import ExitStack

import numpy as np
import ml_dtypes
import concourse.bass as bass
import concourse.mybir as mybir
from concourse.bass_utils import run_bass_kernel_spmd

F32 = mybir.dt.float32
BF16 = mybir.dt.bfloat16
AF = mybir.ActivationFunctionType
ALU = mybir.AluOpType
AX = mybir.AxisListType.X

D = 1024
SEQ = 4096
BATCH = 4
SBT = 512
NOWN = 2048
DFF = 2816
NFC = DFF // 128
EPS = 1e-6
LAMBDA_INIT = 0.8 - 0.6 * math.exp(-0.3 * 0)
NEG = -30000.0
N_DUMMY = 1
N_DUMMY_PROJ = 0


class Sched:
    ENGS = ("pe", "act", "dve", "pool", "sp")

    def __init__(self, nc, stack, n_dma_sems=24):
        self.nc = nc
        self.ops = {e: [] for e in self.ENGS}
        self.cnt = {e: 0 for e in self.ENGS}
        self.seen = {e: {} for e in self.ENGS}
        self.res = {}
        self.sem = {}
        for e in self.ENGS:
            self.sem[e] = stack.enter_context(nc.semaphore("s_" + e))
        self.ndma = n_dma_sems
        for i in range(n_dma_sems):
            self.sem[("d", i)] = stack.enter_context(nc.semaphore("s_d%d" % i))
        self.dval = [0] * n_dma_sems
        self.dnext = 0
        self.n_sw = 0
        self.stack = stack
        self.pending_nosig = {e: False for e in self.ENGS}

    def _r(self, key):
        if key not in self.res:
            self.res[key] = {"w": None, "r": {}}
        return self.res[key]

    def _wait(self, eng, tok):
        if tok is None:
            return
        k, v = tok
        if k == eng == "pe":
            return
        if self.seen[eng].get(k, 0) >= v:
            return
        self.seen[eng][k] = v
        sem = self.sem[k]
        self.ops[eng].append(lambda e, sem=sem, v=v: e.wait_ge(sem, v))

    def _deps(self, eng, reads, writes):
        for r in reads:
            self._wait(eng, self._r(r)["w"])
        for w in writes:
            rr = self._r(w)
            self._wait(eng, rr["w"])
            for k, v in list(rr["r"].items()):
                self._wait(eng, (k, v))

    def _update(self, tok, reads, writes):
        k, v = tok
        for r in reads:
            rr = self._r(r)
            rr["r"][k] = max(rr["r"].get(k, 0), v)
        for w in writes:
            self.res[w] = {"w": tok, "r": {}}

    def op(self, eng, fn, reads=(), writes=(), signal=True):
        self._deps(eng, reads, writes)
        if signal:
            self.cnt[eng] += 1
            tok = (eng, self.cnt[eng])
            sem = self.sem[eng]
            self.ops[eng].append(lambda e, fn=fn, sem=sem: fn(e).then_inc(sem, 1))
            self.pending_nosig[eng] = False
        else:
            tok = (eng, self.cnt[eng] + 1)
            self.ops[eng].append(lambda e, fn=fn: fn(e))
            self.pending_nosig[eng] = True
        self._update(tok, reads, writes)
        return tok

    def group(self, eng, fns, reads=(), writes=()):
        for f in fns[:-1]:
            self.op(eng, f, reads, writes, signal=False)
        return self.op(eng, fns[-1], reads, writes, signal=True)

    def dma(self, out, in_, reads=(), writes=(), q="sp", **kw):
        if q == "pool":
            k = ("w", self.n_sw)
            self.sem[k] = self.stack.enter_context(self.nc.semaphore("s_w%d" % self.n_sw))
            self.n_sw += 1
            self._deps(q, reads, writes)
            tok = (k, 16)
            self.sw_toks = getattr(self, "sw_toks", []) + [tok]
        else:
            i = self.dnext
            self.dnext = (self.dnext + 1) % self.ndma
            k = ("d", i)
            if self.dval[i] > 0:
                self._wait(q, (k, self.dval[i]))
            self._deps(q, reads, writes)
            self.dval[i] += 16
            tok = (k, self.dval[i])
        sem = self.sem[k]
        self.ops[q].append(lambda e, out=out, in_=in_, sem=sem, kw=kw: e.dma_start(out=out, in_=in_, **kw).then_inc(sem, 16))
        self._update(tok, reads, writes)
        return tok

    def fence(self):
        for e in self.ENGS:
            for x in self.ENGS:
                if x != e and self.cnt[x] > 0:
                    self._wait(e, (x, self.cnt[x]))
            for i in range(self.ndma):
                if self.dval[i] > 0:
                    self._wait(e, (("d", i), self.dval[i]))
            for tok in getattr(self, "sw_toks", []):
                self._wait(e, tok)

    def finish(self, eng="sp"):
        for i in range(self.ndma):
            if self.dval[i] > 0:
                self._wait(eng, (("d", i), self.dval[i]))
        for e in self.ENGS:
            assert not self.pending_nosig[e], e
        nc = self.nc
        with nc.Block() as block:
            @block.tensor
            def _(e):
                for f in self.ops["pe"]:
                    f(e)

            @block.scalar
            def _(e):
                for f in self.ops["act"]:
                    f(e)

            @block.vector
            def _(e):
                for f in self.ops["dve"]:
                    f(e)

            @block.gpsimd
            def _(e):
                for f in self.ops["pool"]:
                    f(e)

            @block.sync
            def _(e):
                for f in self.ops["sp"]:
                    f(e)


_UID = [0]


def _uid(name):
    _UID[0] += 1
    return "%s_u%d" % (name, _UID[0])


class Ring:
    def __init__(self, nc, stack, name, shape, dt, n, psum=False):
        alloc = nc.psum_tensor if psum else nc.sbuf_tensor
        self.t = [stack.enter_context(alloc(_uid("rg_%s%d" % (name, i)), shape, dt)) for i in range(n)]
        self.k = ["%s%d" % (name, i) for i in range(n)]
        self.i = 0

    def next(self):
        i = self.i
        self.i = (i + 1) % len(self.t)
        return self.t[i], self.k[i]


class ViewRing:
    def __init__(self, name, views):
        self.t = views
        self.k = ["%s%d" % (name, i) for i in range(len(views))]
        self.i = 0

    def next(self):
        i = self.i
        self.i = (i + 1) % len(self.t)
        return self.t[i], self.k[i]


def MM(out, lhsT, rhs, start=True, stop=True):
    return lambda e: e.matmul(out, lhsT=lhsT, rhs=rhs, start=start, stop=stop, skip_group_check=True)


def TR(out, in_, ident):
    return lambda e: e.transpose(out=out, in_=in_, identity=ident)


def ACT(out, in_, func, **kw):
    return lambda e: e.activation(out=out, in_=in_, func=func, **kw)


def TT(out, a, b, op):
    return lambda e: e.tensor_tensor(out=out, in0=a, in1=b, op=op)


def TS(out, a, s1, s2, op0, op1=None):
    if op1 is None:
        return lambda e: e.tensor_scalar(out=out, in0=a, scalar1=s1, scalar2=None, op0=op0)
    return lambda e: e.tensor_scalar(out=out, in0=a, scalar1=s1, scalar2=s2, op0=op0, op1=op1)


def STT(out, a, s, b, op0, op1):
    return lambda e: e.scalar_tensor_tensor(out=out, in0=a, scalar=s, in1=b, op0=op0, op1=op1)


def CP(out, in_):
    return lambda e: e.tensor_copy(out=out, in_=in_)


def RED(out, in_):
    return lambda e: e.tensor_reduce(out=out, in_=in_, axis=AX, op=ALU.add)


def MS(ap, v):
    return lambda e: e.memset(ap, v)


def build_nc(dbg=None):
    nc = bass.Bass("TRN2", target_bir_lowering=False)

    def din(name, shape):
        return nc.dram_tensor(name, shape, F32, kind="ExternalInput").ap()

    xs = din("xs", [SEQ, D])
    cs_d = din("cs", [128, 32 * 32])
    bg_d = din("bgl", [128, 16])
    gqk_d = din("gqk", [128, 2 * 512])
    lamb_d = din("lamb", [128, 4 * 64])
    vones_d = nc.dram_tensor("vones", [128, 128], BF16, kind="ExternalInput").ap()
    consts_d = nc.dram_tensor("consts", [128, 640], BF16, kind="ExternalInput").ap()
    g_mix = din("g_mix", [1, D])
    w_in = din("w_in", [1, D, 3072])
    g_q = din("g_q", [1, 64])
    g_k = din("g_k", [1, 64])
    lam_q1 = din("lam_q1", [1, 64])
    lam_k1 = din("lam_k1", [1, 64])
    lam_q2 = din("lam_q2", [1, 64])
    lam_k2 = din("lam_k2", [1, 64])
    g_sub = din("g_sub", [1, 128])
    w_ba = din("w_branch_a", [1, 512, D])
    w_bb = din("w_branch_b", [1, 512, D])
    w_gate = din("w_gate", [1, D, 2 * D])
    b_gate = din("b_gate", [1, 2 * D])
    w_out = din("w_out", [1, D, D])
    g_ffn = din("g_ffn", [1, D])
    w_fg = din("w_ffn_gate", [1, D, DFF])
    w_fu = din("w_ffn_up", [1, D, DFF])
    w_fd = din("w_ffn_down", [1, DFF, D])
    out = nc.dram_tensor("out", [NOWN, D], F32, kind="ExternalOutput").ap()
    dbg_t = None
    if dbg in ("da", "sb"):
        dbg_t = nc.dram_tensor("dbg", [128, 4, NOWN], BF16, kind="ExternalOutput").ap()

    with ExitStack() as st:
        S = Sched(nc, st)

        def sbt(stack, name, shape, dt):
            return stack.enter_context(nc.sbuf_tensor(_uid("sb_" + name), shape, dt))

        def pst(stack, name, shape, dt):
            return stack.enter_context(nc.psum_tensor(_uid("pm_" + name), shape, dt))

        cst = sbt(st, "cst", [128, 640], BF16)
        ident = cst[:, 0:128]
        trimask = cst[:, 128:256]
        negtri = cst[:, 256:384]
        negones = cst[:, 384:512]
        ones = cst[:, 512:640]
        vones = sbt(st, "vones", [128, 128], BF16)
        gffn_bc = sbt(st, "gffn_bc", [128, D], F32)
        bgate = sbt(st, "bgate", [128, 16], F32)
        S.dma(cst[:], consts_d, writes=["cst"])
        S.dma(vones[:], vones_d, writes=["vones"])
        S.dma(gffn_bc[:], g_ffn[0, :].partition_broadcast(128), writes=["gffn_bc"])
        S.dma(bgate[:], bg_d, writes=["bgate"], q="act")

        s_att = ExitStack()
        st.enter_context(s_att)
        pers = sbt(s_att, "pers", [128, 8, NOWN], BF16)
        oaT = pers[:, 0:4, :]
        obT = pers[:, 4:8, :]
        scr = pers[:].rearrange("p a b -> p (a b)").bitcast(F32)
        gmix_bc = sbt(s_att, "gmix_bc", [128, D], F32)
        S.dma(gmix_bc[:], g_mix[0, :].partition_broadcast(128), writes=["gmix_bc"])

        def rms_to_hT(R, xt, xk, gbc, gk, hT, hk, tcol, par):
            junk, jk = R["junk"].next()
            ss, sk = R["ss"].next()
            hb, hbk = R["hb"].next()
            pT, pk = R["pT"].next()
            S.op("act", ACT(junk[:], xt, AF.Square, accum_out=ss[:, 0:1]), reads=[xk], writes=[jk, sk])
            S.op("act", ACT(ss[:, 1:2], ss[:, 0:1], AF.Ln, scale=1.0 / D, bias=EPS), reads=[sk], writes=[sk])
            S.op("act", ACT(ss[:, 2:3], ss[:, 1:2], AF.Exp, scale=-0.5), reads=[sk], writes=[sk])
            S.op("dve", STT(hb[:], xt, ss[:, 2:3], gbc[:], ALU.mult, ALU.mult), reads=[xk, sk, gk], writes=[hbk])
            S.group("pe", [TR(pT[:, dc, :], hb[:, dc * 128:(dc + 1) * 128], ident) for dc in range(8)],
                    reads=[hbk, "cst"], writes=[pk])
            eng = "act" if par % 2 == 0 else "dve"
            if eng == "act":
                S.op("act", ACT(hT[:, :, tcol:tcol + 128], pT[:], AF.Copy), reads=[pk], writes=[hk])
            else:
                S.op("dve", CP(hT[:, :, tcol:tcol + 128], pT[:]), reads=[pk], writes=[hk])

        def run_pipeline(jobs):
            active = []
            it = iter(jobs)
            more = True
            while more or active:
                nxt = []
                for g in active:
                    try:
                        next(g)
                        nxt.append(g)
                    except StopIteration:
                        pass
                active = nxt
                if more:
                    try:
                        g = next(it)
                    except StopIteration:
                        more = False
                        continue
                    try:
                        next(g)
                        active.append(g)
                    except StopIteration:
                        pass

        def rms_stat(R, xt, xk):
            junk, jk = R["junk"].next()
            ss, sk = R["ss"].next()
            S.op("act", ACT(junk[:], xt, AF.Square, accum_out=ss[:, 0:1]), reads=[xk], writes=[jk, sk])
            S.op("act", ACT(ss[:, 1:2], ss[:, 0:1], AF.Ln, scale=1.0 / D, bias=EPS), reads=[sk], writes=[sk])
            S.op("act", ACT(ss[:, 2:3], ss[:, 1:2], AF.Exp, scale=-0.5), reads=[sk], writes=[sk])
            return ss, sk

        def rms_apply(R, xt, xk, ss, sk, gbc, gk):
            hb, hbk = R["hb"].next()
            S.op("dve", STT(hb[:], xt, ss[:, 2:3], gbc[:], ALU.mult, ALU.mult), reads=[xk, sk, gk], writes=[hbk])
            return hb, hbk

        def rms_stage(R, xt, xk, gbc, gk):
            ss, sk = rms_stat(R, xt, xk)
            return rms_apply(R, xt, xk, ss, sk, gbc, gk)

        def tr_stage(R, hb, hbk, dst, dk, par):
            pT, pk = R["pT"].next()
            S.group("pe", [TR(pT[:, dc, :], hb[:, dc * 128:(dc + 1) * 128], ident) for dc in range(8)],
                    reads=[hbk, "cst"], writes=[pk])
            if par % 2 == 0:
                S.op("act", ACT(dst, pT[:], AF.Copy), reads=[pk], writes=[dk])
            else:
                S.op("dve", CP(dst, pT[:]), reads=[pk], writes=[dk])

        def att_units(j):
            units = [(j, kb, True, False) for kb in (3, 2, 1, 0)]
            for i in range(j, -1, -1):
                units += [(4 + i, kb, False, i == 0) for kb in (3, 2, 1, 0)]
                if i > 0:
                    units += [(i - 1, kb, False, False) for kb in (3, 2, 1, 0)]
            return units

        with ExitStack() as sda:
            gq_bc = sbt(sda, "gq_bc", [128, 8, 64], F32)
            gk_bc = sbt(sda, "gk_bc", [128, 8, 64], F32)
            cs_all = sbt(sda, "cs_all", [128, 32, 32], F32)
            lamv = sbt(sda, "lamv", [128, 4, 64], F32)
            lamt = sbt(sda, "lamt", [128, 2, 64], F32)
            lams = sbt(sda, "lams", [128, 2], F32)
            lame = sbt(sda, "lame", [128, 2], F32)
            neglam = sbt(sda, "neglam", [128, 1], F32)
            gsub8 = sbt(sda, "gsub8", [128, 1], F32)
            S.dma(gq_bc[:], gqk_d[:, 0:512].rearrange("p (s d) -> p s d", d=64), writes=[("gq_ld", 0)], q="act")
            S.dma(gk_bc[:], gqk_d[:, 512:1024].rearrange("p (s d) -> p s d", d=64), writes=[("gk_ld", 0)], q="act")
            S.dma(cs_all[:], cs_d.rearrange("p (b c) -> p b c", c=32), writes=["cs_all"], q="act")
            S.dma(lamv[:], lamb_d.rearrange("p (i d) -> p i d", d=64), writes=[("lamv", 0)], q="act")
            S.dma(gsub8[:], g_sub[0, :].rearrange("(p o) -> p o", o=1), writes=["gsub8"], q="act")

            def small_param_math():
                lk = [("lamv", 0)]
                S.op("dve", TT(lamt[:, 0, :], lamv[:, 0, :], lamv[:, 1, :], ALU.mult), reads=lk, writes=["lamt"])
                S.op("dve", TT(lamt[:, 1, :], lamv[:, 2, :], lamv[:, 3, :], ALU.mult), reads=lk, writes=["lamt"])
                S.op("dve", RED(lams[:], lamt[:]), reads=["lamt"], writes=["lams"])
                S.op("act", ACT(lame[:], lams[:], AF.Exp), reads=["lams"], writes=["lame"])
                S.op("dve", TT(neglam[:], lame[:, 1:2], lame[:, 0:1], ALU.subtract), reads=["lame"], writes=["neglam"])
                S.op("dve", TS(neglam[:], neglam[:], -LAMBDA_INIT, None, ALU.add), reads=["neglam"], writes=["neglam"])
                S.op("dve", TS(gsub8[:], gsub8[:], 1.0 - LAMBDA_INIT, None, ALU.mult), reads=["gsub8"], writes=["gsub8"])
            KT = sbt(sda, "KT_da", [128, 4, SEQ], BF16)
            QT = sbt(sda, "QT_da", [128, 4, NOWN], BF16)
            Vd = sbt(sda, "V_da", [128, 32, 512], BF16)
            with ExitStack() as sp1:
                Wq = sbt(sp1, "Wdq", [128, 8, 512], BF16)
                Wk = sbt(sp1, "Wdk", [128, 8, 512], BF16)
                Wv = sbt(sp1, "Wdv", [128, 8, 512], BF16)
                S.dma(Wk[:], w_in[0, :, 2048:2560].rearrange("(k p) n -> p k n", p=128), writes=["Wdk"], q="pool")
                S.dma(Wv[:], w_in[0, :, 2560:3072].rearrange("(k p) n -> p k n", p=128), writes=["Wdv"], q="pool")
                S.dma(Wq[:], w_in[0, :, 1536:2048].rearrange("(k p) n -> p k n", p=128), writes=["Wdq"], q="pool")
                S.op("dve", TS(gq_bc[:], gq_bc[:], 0.125, None, ALU.mult), reads=[("gq_ld", 0)], writes=["gq_bc"])
                S.op("dve", TS(gk_bc[:], gk_bc[:], 1.0, None, ALU.mult), reads=[("gk_ld", 0)], writes=["gk_bc"])
                R = {
                    "x": ViewRing("xtv", [scr[:, i * 1024:(i + 1) * 1024] for i in range(4)]),
                    "junk": Ring(nc, sp1, "junk", [128, D], BF16, 1),
                    "ss": Ring(nc, sp1, "ss", [128, 4], F32, 4),
                    "hb": Ring(nc, sp1, "hb", [128, D], BF16, 3),
                    "pT": Ring(nc, sp1, "pT", [128, 8, 128], BF16, 2, psum=True),
                    "hTt": Ring(nc, sp1, "hTt", [128, 8, 128], BF16, 3),
                    "pp": Ring(nc, sp1, "pp", [128, 512], F32, 4, psum=True),
                    "pT2": Ring(nc, sp1, "pTq", [128, 8, 128], BF16, 2, psum=True),
                    "f": ViewRing("qkfv", [scr[:, 4096 + i * 512:4096 + (i + 1) * 512].rearrange("p (s d) -> p s d", d=64) for i in range(8)]),
                    "sq": Ring(nc, sp1, "qksq", [128, 8, 64], F32, 4),
                    "n1": Ring(nc, sp1, "qkn1", [128, 8, 64], F32, 3),
                    "xg": Ring(nc, sp1, "qkxg", [128, 8, 16], F32, 4),
                    "s8": Ring(nc, sp1, "s8", [128, 3, 8], F32, 8),
                    "rt": Ring(nc, sp1, "rt", [128, 2, 8, 16], F32, 3),
                    "b": Ring(nc, sp1, "qkb", [128, 512], BF16, 6),
                }

                def da_proj_job(blk):
                    n, t = divmod(blk, 4)
                    xt, xk = R["x"].next()
                    S.dma(xt, xs[blk * 128:(blk + 1) * 128, :], writes=[xk])
                    yield
                    ss, sk = rms_stat(R, xt, xk)
                    yield
                    hb, hbk = rms_apply(R, xt, xk, ss, sk, gmix_bc, "gmix_bc")
                    yield
                    hTt, hk = R["hTt"].next()
                    tr_stage(R, hb, hbk, hTt[:], hk, 0)
                    yield
                    lhs = [hTt[:, dc, :] for dc in range(8)]
                    paths = []
                    pp, ppk = R["pp"].next()
                    S.group("pe", [MM(pp[:], lhs[dc], Wk[:, dc, :], start=(dc == 0), stop=(dc == 7)) for dc in range(8)],
                            reads=[hk, "Wdk"], writes=[ppk])
                    f, fk = R["f"].next()
                    S.op("act", ACT(f.rearrange("p s d -> p (s d)"), pp[:], AF.Copy), reads=[ppk], writes=[fk])
                    sq, sqk = R["sq"].next()
                    S.op("act", ACT(sq[:], f, AF.Square), reads=[fk], writes=[sqk])
                    paths.append((f, fk, sq, sqk, gk_bc, "gk_bc", KT, ("KT", blk)))
                    pp, ppk = R["pp"].next()
                    S.group("pe", [MM(pp[:], lhs[dc], Wv[:, dc, :], start=(dc == 0), stop=(dc == 7)) for dc in range(8)],
                            reads=[hk, "Wdv"], writes=[ppk])
                    S.op("act", ACT(Vd[:, blk, :], pp[:], AF.Copy), reads=[ppk], writes=[("Vd", blk)])
                    if n < 4:
                        pp, ppk = R["pp"].next()
                        S.group("pe", [MM(pp[:], lhs[dc], Wq[:, dc, :], start=(dc == 0), stop=(dc == 7)) for dc in range(8)],
                                reads=[hk, "Wdq"], writes=[ppk])
                        f, fk = R["f"].next()
                        S.op("act", ACT(f.rearrange("p s d -> p (s d)"), pp[:], AF.Copy), reads=[ppk], writes=[fk])
                        sq, sqk = R["sq"].next()
                        S.op("act", ACT(sq[:], f, AF.Square), reads=[fk], writes=[sqk])
                        paths.append((f, fk, sq, sqk, gq_bc, "gq_bc", QT, ("QT", blk)))
                    yield
                    st1 = []
                    for (f, fk, sq, sqk, gbc, gkey, dstT, dkey) in paths:
                        s8, s8k = R["s8"].next()
                        S.op("dve", RED(s8[:, 0, :], sq[:]), reads=[sqk], writes=[s8k])
                        st1.append((f, fk, gbc, gkey, dstT, dkey, s8, s8k))
                    yield
                    for (f, fk, gbc, gkey, dstT, dkey, s8, s8k) in st1:
                        S.op("act", ACT(s8[:, 1, :], s8[:, 0, :], AF.Ln, scale=1.0 / 64, bias=EPS), reads=[s8k], writes=[s8k])
                        S.op("act", ACT(s8[:, 2, :], s8[:, 1, :], AF.Exp, scale=-0.5), reads=[s8k], writes=[s8k])
                    yield
                    st2 = []
                    for (f, fk, gbc, gkey, dstT, dkey, s8, s8k) in st1:
                        n1, n1k = R["n1"].next()
                        b, bk = R["b"].next()
                        xg, xgk = R["xg"].next()
                        bv = b[:].rearrange("p (s d) -> p s d", d=64)
                        S.op("dve", TT(n1[:], f, s8[:, 2, :].to_broadcast([128, 8, 64]), ALU.mult), reads=[fk, s8k], writes=[n1k])
                        S.op("dve", TT(bv, n1[:], gbc[:], ALU.mult), reads=[n1k, gkey], writes=[bk])
                        S.op("dve", TT(xg[:], n1[:, :, 0:16], gbc[:, :, 0:16], ALU.mult), reads=[n1k, gkey], writes=[xgk])
                        st2.append((xg, xgk, b, bk, bv, dstT, dkey))
                    yield
                    st3 = []
                    for (xg, xgk, b, bk, bv, dstT, dkey) in st2:
                        rt, rtk = R["rt"].next()
                        ccb = cs_all[:, blk, 0:16].unsqueeze(1).to_broadcast([128, 8, 16])
                        ssb_ = cs_all[:, blk, 16:32].unsqueeze(1).to_broadcast([128, 8, 16])
                        S.op("dve", TT(rt[:, 0], xg[:], ccb, ALU.mult), reads=[xgk, "cs_all"], writes=[rtk])
                        S.op("dve", TT(rt[:, 1], xg[:], ssb_, ALU.mult), reads=[xgk, "cs_all"], writes=[rtk])
                        S.op("dve", TT(bv[:, :, 0:8], rt[:, 0, :, 0:8], rt[:, 1, :, 8:16], ALU.subtract), reads=[rtk], writes=[bk])
                        S.op("dve", TT(bv[:, :, 8:16], rt[:, 0, :, 8:16], rt[:, 1, :, 0:8], ALU.add), reads=[rtk], writes=[bk])
                        st3.append((b, bk, dstT, dkey))
                    yield
                    for pi, (b, bk, dstT, dkey) in enumerate(st3):
                        pT2, pT2k = R["pT2"].next()
                        S.group("pe", [TR(pT2[:, hh, :], b[:, hh * 128:(hh + 1) * 128], ident) for hh in range(4)],
                                reads=[bk, "cst"], writes=[pT2k])
                        dcol = blk * 128
                        S.op("act", ACT(dstT[:, :, dcol:dcol + 128], pT2[:, 0:4, :], AF.Copy), reads=[pT2k], writes=[dkey])

                run_pipeline([da_proj_job(blk) for blk in range(32)])

            small_param_math()
            S.fence()
            with ExitStack() as sp2:
                PS2 = Ring(nc, sp2, "ps_s", [128, 2, 512], F32, 2, psum=True)
                o1 = pst(sp2, "ps_o1", [128, 512], F32)
                o2 = pst(sp2, "ps_o2", [128, 512], F32)
                lacc = pst(sp2, "ps_lacc", [128, 512], F32)
                sel = sbt(sp2, "sel", [128, 2, 128], BF16)
                rhl = sbt(sp2, "rhl", [128, 2, 512], BF16)
                S.op("pool", MS(sel[:], 0.0), writes=["sel"])
                S.op("pool", MS(sel[0:1, 0, :], 1.0), writes=["sel"])
                S.op("pool", MS(sel[64:65, 1, :], 1.0), writes=["sel"])
                PB = Ring(nc, sp2, "Pb", [128, 2, 512], BF16, 4)
                EP = {k: sbt(sp2, "ep_" + k, [128, 512], F32) for k in ("t1", "r1", "t2", "r2", "c1", "c2", "a1", "a2", "ob", "ln", "rs")}
                sqb = sbt(sp2, "ep_sq", [128, 512], BF16)

                da_pend = []
                da_later = []
                epb = pst(sp2, "ps_ep", [128, 512], F32)

                def da_att_job(h, j, u, nun, nst, kb, diag, usev):
                    blk = nst * 4 + kb
                    c0 = kb * 128 if diag else 0
                    q0 = j * SBT + c0
                    q1 = (j + 1) * SBT
                    sp_, spk = PS2.next()
                    qkeys = [("QT", j * 4 + tt) for tt in range(4)]
                    S.group("pe", [MM(sp_[:, 0, c0:], KT[0:64, h, blk * 128:(blk + 1) * 128], QT[0:64, h, q0:q1]),
                                   MM(sp_[:, 1, c0:], KT[64:128, h, blk * 128:(blk + 1) * 128], QT[64:128, h, q0:q1])],
                            reads=[("KT", blk)] + qkeys, writes=[spk])
                    pb, pbk = PB.next()
                    S.op("act", ACT(pb[:, :, c0:], sp_[:, :, c0:], AF.Exp), reads=[spk], writes=[pbk])
                    if diag:
                        S.op("pool", MS(pb[64:128, :, c0:c0 + 64], 0.0), reads=[], writes=[pbk])

                    def emit_pv(h=h, j=j, u=u, nun=nun, blk=blk, c0=c0, usev=usev, pb=pb, pbk=pbk):
                        _emit_pv(h, j, u, nun, blk, c0, usev, pb, pbk)

                    da_pend.append(emit_pv)
                    if len(da_pend) > 2:
                        da_pend.pop(0)()
                    da_tick()
                    return
                    yield

                def _emit_pv(h, j, u, nun, blk, c0, usev, pb, pbk):
                    lo = vones[:] if usev else ones
                    lok = "vones" if usev else "cst"
                    first = (u == 0)
                    last = (u == nun - 1)
                    vv = Vd[:, blk, h * 128:(h + 1) * 128]
                    S.group("pe", [MM(o1[:, c0:], vv, pb[:, 0, c0:], start=first, stop=last),
                                   MM(o2[:, c0:], vv, pb[:, 1, c0:], start=first, stop=last),
                                   MM(lacc[0:64, c0:], lo[:, 0:64], pb[:, 0, c0:], start=first, stop=last),
                                   MM(lacc[64:128, c0:], lo[:, 0:64], pb[:, 1, c0:], start=first, stop=last)],
                            reads=[("Vd", blk), pbk, lok], writes=["o1", "lacc", "o2"])
                    if not last:
                        return
                    E = EP

                    def e1():
                        S.op("act", ACT(E["t1"][:], lacc[:], AF.Ln), reads=["lacc"], writes=["e_t1"])
                        S.op("dve", CP(E["c1"][:], o1[:]), reads=["o1"], writes=["e_c1"])
                        S.op("dve", CP(E["c2"][:], o2[:]), reads=["o2"], writes=["e_c2"])
                        S.op("act", ACT(E["r1"][:], E["t1"][:], AF.Exp, scale=-1.0), reads=["e_t1"], writes=["e_r1"])
                        S.op("dve", CP(rhl[:, 0, :], E["r1"][:]), reads=["e_r1"], writes=["rhl"])
                        S.op("dve", TT(rhl[:, 1, :], E["r1"][:], rhl[:, 0, :], ALU.subtract), reads=["e_r1", "rhl"], writes=["rhl"])

                    def e2():
                        S.group("pe", [MM(epb[:], sel[:, 0, :], rhl[:, 0, :], start=True, stop=False),
                                       MM(epb[:], sel[:, 0, :], rhl[:, 1, :], start=False, stop=True)],
                                reads=["sel", "rhl"], writes=["epb"])
                        S.op("dve", TT(E["a1"][:], E["c1"][:], epb[:], ALU.mult), reads=["e_c1", "epb"], writes=["e_a1"])
                        S.group("pe", [MM(epb[:], sel[:, 1, :], rhl[:, 0, :], start=True, stop=False),
                                       MM(epb[:], sel[:, 1, :], rhl[:, 1, :], start=False, stop=True)],
                                reads=["sel", "rhl"], writes=["epb"])
                        S.op("dve", STT(E["a2"][:], E["c2"][:], neglam[:, 0:1], epb[:], ALU.mult, ALU.mult),
                             reads=["e_c2", "epb", "neglam"], writes=["e_a2"])
                        S.op("dve", TT(E["ob"][:], E["a1"][:], E["a2"][:], ALU.add), reads=["e_a1", "e_a2"], writes=["e_ob"])
                        S.op("act", ACT(sqb[:], E["ob"][:], AF.Square), reads=["e_ob"], writes=["e_sq"])

                    def e3(h=h, j=j):
                        S.op("pe", MM(epb[:], ones, sqb[:]), reads=["e_sq", "cst"], writes=["epb"])
                        S.op("act", ACT(E["ln"][:], epb[:], AF.Ln, scale=1.0 / 128, bias=EPS), reads=["epb"], writes=["e_ln"])
                        S.op("act", ACT(E["rs"][:], E["ln"][:], AF.Exp, scale=-0.5), reads=["e_ln"], writes=["e_rs"])
                        S.op("dve", STT(obT[:, h, j * SBT:(j + 1) * SBT], E["ob"][:], gsub8[:, 0:1], E["rs"][:], ALU.mult, ALU.mult),
                             reads=["e_ob", "e_rs", "gsub8"], writes=[("obT", h, j)])

                    e1()
                    da_later.append([2, e2])
                    da_later.append([5, e3])

                def da_tick():
                    for item in list(da_later):
                        item[0] -= 1
                        if item[0] <= 0:
                            da_later.remove(item)
                            item[1]()

                jobs = []
                for h in range(4):
                    for j in range(4):
                        units = att_units(j)
                        for u, (nst, kb, diag, usev) in enumerate(units):
                            jobs.append(da_att_job(h, j, u, len(units), nst, kb, diag, usev))
                run_pipeline(jobs)
                while da_pend:
                    da_pend.pop(0)()
                    da_tick()
                while da_later:
                    da_tick()

        S.fence()
        if dbg == "da":
            S.dma(dbg_t, obT, reads=[("obT", h, j) for h in range(4) for j in range(4)])
            S.finish()
            return nc

        s_sbo = ExitStack()
        s_att.enter_context(s_sbo)
        hTown = [sbt(s_sbo, "hTown%d" % n, [128, 8, SBT], BF16) for n in range(4)]
        with ExitStack() as ssb:
            KTs = sbt(ssb, "KT_sb", [128, 4, SEQ], BF16)
            QTs = sbt(ssb, "QT_sb", [128, 4, NOWN], BF16)
            Vs = sbt(ssb, "V_sb", [128, 32, 512], BF16)
            with ExitStack() as sp1:
                Wq = sbt(sp1, "Wsq", [128, 8, 512], BF16)
                Wk = sbt(sp1, "Wsk", [128, 8, 512], BF16)
                Wv = sbt(sp1, "Wsv", [128, 8, 512], BF16)
                S.dma(Wk[:], w_in[0, :, 512:1024].rearrange("(k p) n -> p k n", p=128), writes=["Wsk"], q="pool")
                S.dma(Wq[:], w_in[0, :, 0:512].rearrange("(k p) n -> p k n", p=128), writes=["Wsq"], q="pool")
                S.dma(Wv[:], w_in[0, :, 1024:1536].rearrange("(k p) n -> p k n", p=128), writes=["Wsv"], q="pool")
                R = {
                    "x": ViewRing("xtv2", [scr[:, i * 1024:(i + 1) * 1024] for i in range(4)]),
                    "junk": Ring(nc, sp1, "junk", [128, D], BF16, 1),
                    "ss": Ring(nc, sp1, "ss", [128, 4], F32, 4),
                    "hb": Ring(nc, sp1, "hb", [128, D], BF16, 3),
                    "pT": Ring(nc, sp1, "pT", [128, 8, 128], BF16, 2, psum=True),
                    "hT": Ring(nc, sp1, "hT", [128, 8, SBT], BF16, 2),
                    "pp": Ring(nc, sp1, "pp", [128, 512], F32, 4, psum=True),
                }
                sbh = {}
                dmy_sp = pst(sp1, "ps_dmysp", [128, 512], F32)

                def sb_proj_job(blk):
                    n, t = divmod(blk, 4)
                    if t == 0:
                        sbh[n] = (hTown[n], "hTown%d" % n) if n < 4 else R["hT"].next()
                    xt, xk = R["x"].next()
                    S.dma(xt, xs[blk * 128:(blk + 1) * 128, :], writes=[xk])
                    yield
                    ss, sk = rms_stat(R, xt, xk)
                    yield
                    hb, hbk = rms_apply(R, xt, xk, ss, sk, gmix_bc, "gmix_bc")
                    yield
                    hT, hk = sbh[n]
                    for _ in range(N_DUMMY_PROJ):
                        S.op("pe", MM(dmy_sp[:], ident, Wk[:, 0, :]), reads=["cst", "Wsk"], writes=[], signal=False)
                    tr_stage(R, hb, hbk, hT[:, :, t * 128:(t + 1) * 128], hk, blk)
                    if t != 3:
                        return
                    yield
                    par = 0
                    for p in range(4):
                        if p == 2:
                            yield
                        pp, ppk = R["pp"].next()
                        S.group("pe", [MM(pp[:], Wk[:, dc, p * 128:(p + 1) * 128], hT[:, dc, :], start=(dc == 0), stop=(dc == 7))
                                       for dc in range(8)], reads=[hk, "Wsk"], writes=[ppk])
                        par += 1
                        if par % 2 == 0:
                            S.op("act", ACT(KTs[:, p, n * SBT:(n + 1) * SBT], pp[:], AF.Copy), reads=[ppk], writes=[("KTs", n, p)])
                        else:
                            S.op("dve", CP(KTs[:, p, n * SBT:(n + 1) * SBT], pp[:]), reads=[ppk], writes=[("KTs", n, p)])
                        if n < 4:
                            pp, ppk = R["pp"].next()
                            S.group("pe", [MM(pp[:], Wq[:, dc, p * 128:(p + 1) * 128], hT[:, dc, :], start=(dc == 0), stop=(dc == 7))
                                           for dc in range(8)], reads=[hk, "Wsq"], writes=[ppk])
                            par += 1
                            if par % 2 == 0:
                                S.op("act", ACT(QTs[:, p, n * SBT:(n + 1) * SBT], pp[:], AF.Copy, scale=0.125),
                                     reads=[ppk], writes=[("QTs", n, p)])
                            else:
                                S.op("dve", TS(QTs[:, p, n * SBT:(n + 1) * SBT], pp[:], 0.125, None, ALU.mult),
                                     reads=[ppk], writes=[("QTs", n, p)])
                    for tt in range(4):
                        if tt % 2 == 0:
                            yield
                        bb = n * 4 + tt
                        pp, ppk = R["pp"].next()
                        S.group("pe", [MM(pp[:], hT[:, dc, tt * 128:(tt + 1) * 128], Wv[:, dc, :], start=(dc == 0), stop=(dc == 7))
                                       for dc in range(8)], reads=[hk, "Wsv"], writes=[ppk])
                        par += 1
                        if par % 2 == 0:
                            S.op("act", ACT(Vs[:, bb, :], pp[:], AF.Copy), reads=[ppk], writes=[("Vs", bb)])
                        else:
                            S.op("dve", CP(Vs[:, bb, :], pp[:]), reads=[ppk], writes=[("Vs", bb)])

                run_pipeline([sb_proj_job(blk) for blk in range(32)])

            S.fence()
            with ExitStack() as sp2:
                ZR = Ring(nc, sp2, "ps_z", [128, 2, 512], F32, 1, psum=True)
                PR = Ring(nc, sp2, "ps_p", [128, 2, 512], F32, 2, psum=True)
                OA = Ring(nc, sp2, "ps_oa", [128, 512], F32, 1, psum=True)
                dmy = pst(sp2, "ps_dmy", [128, 512], F32)
                EF = Ring(nc, sp2, "ef", [128, 2, 512], F32, 2)
                SPR = Ring(nc, sp2, "spb", [128, 2, 512], BF16, 5)
                AR = Ring(nc, sp2, "Ab", [128, 2, 512], BF16, 3)
                SRR = Ring(nc, sp2, "srun", [128, 2, 512], BF16, 4)

                def sb_att_job(ctx, p, j, u, nun, nst, kb, diag):
                    blk = nst * 4 + kb
                    c0 = kb * 128 if diag else 0
                    q0 = j * SBT + c0
                    q1 = (j + 1) * SBT
                    if u == 0:
                        ctx["oacc"] = OA.next()
                        ctx["sr"] = [SRR.next(), SRR.next()]
                        for (t_, k_) in ctx["sr"]:
                            S.op("pool", MS(t_[:], 0.0), writes=[k_])
                    kq_keys = [("KTs", nst, p), ("QTs", j, p), "cst"]
                    kts = [KTs[e * 64:e * 64 + 64, p, blk * 128:(blk + 1) * 128] for e in range(2)]
                    qts = [QTs[e * 64:e * 64 + 64, p, q0:q1] for e in range(2)]
                    zp, zk = ZR.next()
                    fns = []
                    for e in range(2):
                        fns.append(MM(zp[:, e, c0:], kts[e], qts[e], start=True, stop=not diag))
                        if diag:
                            fns.append(MM(zp[:, e, c0:c0 + 128], ident, trimask, start=False, stop=True))
                    S.group("pe", fns, reads=kq_keys, writes=[zk])
                    for _ in range(N_DUMMY):
                        S.op("pe", MM(dmy[:], ident, Vs[:, 0, :]), reads=["cst"], writes=[], signal=False)
                    ef, efk = EF.next()
                    sp, spk = SPR.next()
                    S.op("act", ACT(ef[:, :, c0:], zp[:, :, c0:], AF.Exp), reads=[zk], writes=[efk])
                    S.op("act", ACT(sp[:, :, c0:], ef[:, :, c0:], AF.Ln, bias=1.0), reads=[efk], writes=[spk])
                    yield
                    yield
                    (so, sok), (sn, snk) = ctx["sr"][u % 2], ctx["sr"][1 - u % 2]
                    Pp, Pk = PR.next()
                    fns = []
                    for e in range(2):
                        fns.append(MM(Pp[:, e, c0:], kts[e], qts[e], start=True, stop=False))
                        if diag:
                            fns.append(MM(Pp[:, e, c0:c0 + 128], ident, trimask, start=False, stop=False))
                        fns.append(MM(Pp[:, e, c0:], negtri, sp[:, e, c0:], start=False, stop=False))
                        fns.append(MM(Pp[:, e, c0:], negones, so[:, e, c0:], start=False, stop=True))
                    S.group("pe", fns, reads=kq_keys + [spk, sok], writes=[Pk])
                    A, Ak = AR.next()
                    S.op("act", ACT(A[:, :, c0:], Pp[:, :, c0:], AF.Exp), reads=[Pk], writes=[Ak])
                    S.op("dve", TT(sn[:, :, c0:], so[:, :, c0:], sp[:, :, c0:], ALU.add), reads=[sok, spk], writes=[snk])

                    def emit_av(ctx=ctx, p=p, j=j, u=u, nun=nun, blk=blk, c0=c0, A=A, Ak=Ak):
                        oacc, oak = ctx["oacc"]
                        S.group("pe", [MM(oacc[e * 64:e * 64 + 64, c0:], Vs[:, blk, (2 * p + e) * 64:(2 * p + e + 1) * 64], A[:, e, c0:],
                                          start=(u == 0), stop=(u == nun - 1)) for e in range(2)],
                                reads=[("Vs", blk), Ak], writes=[oak])
                        if u == nun - 1:
                            S.op("dve", CP(oaT[:, p, j * SBT:(j + 1) * SBT], oacc[:]), reads=[oak], writes=[("oaT", p, j)])

                    pend = pending_av[0]
                    pending_av[0] = emit_av
                    if pend is not None:
                        pend()

                pending_av = [None]
                jobs = []
                for p in range(4):
                    for j in range(4):
                        units = att_units(j)
                        ctx = {}
                        for u, (nst, kb, diag, _) in enumerate(units):
                            jobs.append(sb_att_job(ctx, p, j, u, len(units), nst, kb, diag))
                run_pipeline(jobs)
                pending_av[0]()

        S.fence()
        if dbg == "sb":
            S.dma(dbg_t, oaT, reads=[("oaT", p, j) for p in range(4) for j in range(4)])
            S.finish()
            return nc

        with ExitStack() as smg:
            Wg = sbt(smg, "Wg", [128, 8, 2 * D], BF16)
            Wo = sbt(smg, "Wo", [128, 8, D], BF16)
            Wba = sbt(smg, "Wba", [128, 4, D], BF16)
            Wbb = sbt(smg, "Wbb", [128, 4, D], BF16)
            for g4 in range(4):
                if g4 == 1:
                    S.dma(Wba[:], w_ba[0].rearrange("(k p) n -> p k n", p=128), writes=["Wba"], q="pool")
                    S.dma(Wbb[:], w_bb[0].rearrange("(k p) n -> p k n", p=128), writes=["Wbb"], q="pool")
                for half in range(2):
                    c0_ = half * D + g4 * 256
                    S.dma(Wg[:, :, c0_:c0_ + 256], w_gate[0, :, c0_:c0_ + 256].rearrange("(k p) n -> p k n", p=128),
                          writes=[("Wg", half, g4)], q="pool")
            S.dma(Wo[:], w_out[0].rearrange("(k p) n -> p k n", p=128), writes=["Wo"], q="pool")
            PG = Ring(nc, smg, "ps_g", [128, 512], F32, 6, psum=True)
            PY = Ring(nc, smg, "ps_y", [128, 512], F32, 2, psum=True)
            GS = Ring(nc, smg, "gs", [128, 512], BF16, 6)
            T12 = Ring(nc, smg, "t12", [128, 512], F32, 4)
            MT = Ring(nc, smg, "mT", [128, 8, SBT], BF16, 2)
            XR = Ring(nc, smg, "xr", [128, D], F32, 2)
            X1 = Ring(nc, smg, "x1r", [128, D], F32, 2)
            mts = {}

            def merge_ec_job(n, ec):
                if ec == 0:
                    mts[n] = MT.next()
                mT, mk = mts[n]
                hT, hk = hTown[n], "hTown%d" % n
                tok = slice(n * SBT, (n + 1) * SBT)
                g4 = ec // 2
                g0, g0k = PG.next()
                S.group("pe", [MM(g0[:], Wg[:, dc, ec * 128:(ec + 1) * 128], hT[:, dc, :], start=(dc == 0), stop=(dc == 7))
                               for dc in range(8)], reads=[hk, ("Wg", 0, g4)], writes=[g0k])
                gs0, gs0k = GS.next()
                S.op("act", ACT(gs0[:], g0[:], AF.Sigmoid, bias=bgate[:, ec:ec + 1]), reads=[g0k, "bgate"], writes=[gs0k])
                g1, g1k = PG.next()
                S.group("pe", [MM(g1[:], Wg[:, dc, D + ec * 128:D + (ec + 1) * 128], hT[:, dc, :], start=(dc == 0), stop=(dc == 7))
                               for dc in range(8)], reads=[hk, ("Wg", 1, g4)], writes=[g1k])
                gs1, gs1k = GS.next()
                S.op("act", ACT(gs1[:], g1[:], AF.Sigmoid, bias=bgate[:, 8 + ec:9 + ec]), reads=[g1k, "bgate"], writes=[gs1k])
                ba, bak = PG.next()
                S.group("pe", [MM(ba[:], Wba[:, cc, ec * 128:(ec + 1) * 128], oaT[:, cc, tok], start=(cc == 0), stop=(cc == 3))
                               for cc in range(4)], reads=["Wba"] + [("oaT", cc, n) for cc in range(4)], writes=[bak])
                bb, bbk = PG.next()
                S.group("pe", [MM(bb[:], Wbb[:, cc, ec * 128:(ec + 1) * 128], obT[:, cc, tok], start=(cc == 0), stop=(cc == 3))
                               for cc in range(4)], reads=["Wbb"] + [("obT", cc, n) for cc in range(4)], writes=[bbk])
                yield
                t1, t1k = T12.next()
                t2, t2k = T12.next()
                S.op("dve", TT(t1[:], ba[:], gs0[:], ALU.mult), reads=[bak, gs0k], writes=[t1k])
                S.op("dve", TT(t2[:], bb[:], gs1[:], ALU.mult), reads=[bbk, gs1k], writes=[t2k])
                S.op("pool", TT(mT[:, ec, :], t1[:], t2[:], ALU.add), reads=[t1k, t2k], writes=[(mk, ec)])

            def merge_y_job(n, t):
                mT, mk = mts[n]
                blk = n * 4 + t
                xt, xk = XR.next()
                S.dma(xt[:], xs[blk * 128:(blk + 1) * 128, :], writes=[xk])
                x1, x1k = X1.next()
                yield
                for hh in range(2):
                    y, yk = PY.next()
                    S.group("pe", [MM(y[:], mT[:, cc, t * 128:(t + 1) * 128], Wo[:, cc, hh * 512:(hh + 1) * 512],
                                      start=(cc == 0), stop=(cc == 7)) for cc in range(8)],
                            reads=[(mk, cc) for cc in range(8)] + ["Wo"], writes=[yk])
                    S.op("dve", TT(x1[:, hh * 512:(hh + 1) * 512], y[:], xt[:, hh * 512:(hh + 1) * 512], ALU.add),
                         reads=[yk, xk], writes=[x1k])
                S.dma(out[blk * 128:(blk + 1) * 128, :], x1[:], reads=[x1k], writes=[("out", blk)])

            jobs = []
            for n in range(4):
                jobs += [merge_ec_job(n, ec) for ec in range(8)]
                jobs += [merge_y_job(n, t) for t in range(4)]
            run_pipeline(jobs)

        s_att.close()
        S.fence()
        if dbg == "merge":
            S.finish()
            return nc

        with ExitStack() as sff:
            Wfg = sbt(sff, "Wfg", [128, 8, DFF], BF16)
            Wfu = sbt(sff, "Wfu", [128, 8, DFF], BF16)
            Wfd = sbt(sff, "Wfd", [128, NFC, D], BF16)
            FCG = [2, 4, 5, 5, 6]
            fc_lo = [sum(FCG[:i]) for i in range(len(FCG))]
            fc_slice = {}
            for k4, (lo_, n_) in enumerate(zip(fc_lo, FCG)):
                for fc_ in range(lo_, lo_ + n_):
                    fc_slice[fc_] = k4
                cs_ = slice(lo_ * 128, (lo_ + n_) * 128)
                S.dma(Wfg[:, :, cs_], w_fg[0, :, cs_].rearrange("(k p) n -> p k n", p=128), writes=[("Wfg", k4)], q="pool")
                S.dma(Wfu[:, :, cs_], w_fu[0, :, cs_].rearrange("(k p) n -> p k n", p=128), writes=[("Wfu", k4)], q="pool")
            S.dma(Wfd[:, 0:11, :], w_fd[0, 0:1408, :].rearrange("(k p) n -> p k n", p=128), writes=["Wfd0"], q="pool")
            S.dma(Wfd[:, 11:22, :], w_fd[0, 1408:2816, :].rearrange("(k p) n -> p k n", p=128), writes=["Wfd1"], q="pool")
            R = {
                "junk": Ring(nc, sff, "junk", [128, D], BF16, 1),
                "ss": Ring(nc, sff, "ss", [128, 4], F32, 3),
                "hb": Ring(nc, sff, "hb", [128, D], BF16, 2),
                "pT": Ring(nc, sff, "pT", [128, 8, 128], BF16, 2, psum=True),
            }
            H2 = Ring(nc, sff, "h2T", [128, 8, SBT], BF16, 2)
            XA = Ring(nc, sff, "x1a", [128, D], F32, 2)
            XB = Ring(nc, sff, "x1b", [128, D], F32, 3)
            FF = sbt(sff, "ffT", [128, NFC, SBT], BF16)
            SG = Ring(nc, sff, "sg", [128, 512], BF16, 3)
            PGU = Ring(nc, sff, "ps_gu", [128, 512], F32, 4, psum=True)
            PD = Ring(nc, sff, "ps_d", [128, 512], F32, 2, psum=True)
            h2s = {}
            mhalf = sbt(sff, "mhalf", [128, 1], F32)
            S.op("pool", MS(mhalf[:], -0.5), writes=["mhalf"])

            def ffn_tile_job(n, t):
                if t == 0:
                    h2s[n] = H2.next()
                h2, h2k = h2s[n]
                blk = n * 4 + t
                xa, xak = XA.next()
                S.dma(xa[:], out[blk * 128:(blk + 1) * 128, :], reads=[("out", blk)], writes=[xak])
                junk, jk = R["junk"].next()
                ss, sk = R["ss"].next()
                S.op("act", ACT(junk[:], xa[:], AF.Square, accum_out=ss[:, 0:1]), reads=[xak], writes=[jk, sk])
                S.op("pool", TS(ss[:, 1:2], ss[:, 0:1], 1.0 / D, EPS, ALU.mult, ALU.add), reads=[sk], writes=[sk])
                S.op("pool", TT(ss[:, 2:3], ss[:, 1:2], mhalf[:, 0:1], ALU.pow), reads=[sk, "mhalf"], writes=[sk])
                hb, hbk = rms_apply(R, xa[:], xak, ss, sk, gffn_bc, "gffn_bc")
                yield
                tr_stage(R, hb, hbk, h2[:, :, t * 128:(t + 1) * 128], (h2k, t), blk)

            def ffn_fc_job(n, fc):
                h2, h2k = h2s[n]
                hkeys = [(h2k, t) for t in range(4)]
                wg_keys = [("Wfg", fc_slice[fc])]
                wu_keys = [("Wfu", fc_slice[fc])]
                pg, pgk = PGU.next()
                S.group("pe", [MM(pg[:], Wfg[:, dc, fc * 128:(fc + 1) * 128], h2[:, dc, :], start=(dc == 0), stop=(dc == 7))
                               for dc in range(8)], reads=hkeys + wg_keys, writes=[pgk])
                sg, sgk = SG.next()
                S.op("act", ACT(sg[:], pg[:], AF.Silu), reads=[pgk], writes=[sgk])
                pu, puk = PGU.next()
                S.group("pe", [MM(pu[:], Wfu[:, dc, fc * 128:(fc + 1) * 128], h2[:, dc, :], start=(dc == 0), stop=(dc == 7))
                               for dc in range(8)], reads=hkeys + wu_keys, writes=[puk])
                yield
                S.op("dve", TT(FF[:, fc, :], pu[:], sg[:], ALU.mult), reads=[puk, sgk], writes=[("ff", fc)])

            def ffn_down_job(n, t):
                blk = n * 4 + t
                xb, xbk = XB.next()
                S.dma(xb[:], out[blk * 128:(blk + 1) * 128, :], reads=[("out", blk)], writes=[xbk])
                yield
                for hh in range(2):
                    pd, pdk = PD.next()
                    S.group("pe", [MM(pd[:], FF[:, fc, t * 128:(t + 1) * 128], Wfd[:, fc, hh * 512:(hh + 1) * 512],
                                      start=(fc == 0), stop=(fc == NFC - 1)) for fc in range(NFC)],
                            reads=[("ff", fc) for fc in range(NFC)] + ["Wfd0", "Wfd1"], writes=[pdk])
                    S.op("dve", TT(xb[:, hh * 512:(hh + 1) * 512], pd[:], xb[:, hh * 512:(hh + 1) * 512], ALU.add),
                         reads=[pdk, xbk], writes=[xbk])
                S.dma(out[blk * 128:(blk + 1) * 128, :], xb[:], reads=[xbk], writes=[("out", blk)])

            jobs = [ffn_tile_job(0, t) for t in range(4)]
            for n in range(4):
                jobs += [ffn_fc_job(n, fc) for fc in range(NFC)]
                if n + 1 < 4:
                    jobs += [ffn_tile_job(n + 1, t) for t in range(4)]
                jobs += [ffn_down_job(n, t) for t in range(4)]
            run_pipeline(jobs)
        S.finish()
    return nc


def _host_consts():
    s = np.arange(128)[:, None]
    t = np.arange(128)[None, :]
    ident = (s == t).astype(np.float32)
    trimask = np.where(s >= t, NEG, 0.0).astype(np.float32)
    negtri = np.where(s >= t, -1.0, 0.0).astype(np.float32)
    negones = -np.ones((128, 128), np.float32)
    ones = np.ones((128, 128), np.float32)
    return np.concatenate([ident, trimask, negtri, negones, ones], axis=1)


def _core_inputs(inputs, b, c):
    x = np.asarray(inputs["x"], dtype=np.float32)
    xs = np.zeros((SEQ, D), np.float32)
    pos = np.zeros((SEQ,), np.float32)
    for i in range(4):
        o = 2 * i + c
        xs[i * SBT:(i + 1) * SBT] = x[b, o * SBT:(o + 1) * SBT]
        pos[i * SBT:(i + 1) * SBT] = np.arange(o * SBT, (o + 1) * SBT)
        xo = o - 1
        if xo >= 0:
            xs[(4 + i) * SBT:(5 + i) * SBT] = x[b, xo * SBT:(xo + 1) * SBT]
            pos[(4 + i) * SBT:(5 + i) * SBT] = np.arange(xo * SBT, (xo + 1) * SBT)
    inv_freq = (np.float32(500000.0) ** (-np.arange(0, 16, 2, dtype=np.float32) / np.float32(16))).astype(np.float32)
    ang = (pos[:, None] * inv_freq[None, :]).astype(np.float32)
    cs = np.concatenate([np.cos(ang), np.cos(ang), np.sin(ang), np.sin(ang)], axis=1).astype(np.float32)
    vones = np.full((128, 128), 1.0 if c == 1 else 0.0, np.float32).astype(ml_dtypes.bfloat16)
    cs = np.ascontiguousarray(cs.reshape(32, 128, 32).transpose(1, 0, 2).reshape(128, 32 * 32))
    bgl = np.ascontiguousarray(np.asarray(inputs["b_gate"], dtype=np.float32)[0].reshape(16, 128).T)
    f32 = lambda k: np.asarray(inputs[k], dtype=np.float32)[0]
    gqk = np.ascontiguousarray(np.broadcast_to(np.concatenate([np.tile(f32("g_q"), 8), np.tile(f32("g_k"), 8)])[None, :], (128, 1024)))
    lamb = np.ascontiguousarray(np.broadcast_to(np.concatenate([f32("lam_q1"), f32("lam_k1"), f32("lam_q2"), f32("lam_k2")])[None, :], (128, 256)))
    m = {"xs": xs, "cs": cs, "bgl": bgl, "gqk": gqk, "lamb": lamb, "vones": vones, "consts": _host_consts().astype(ml_dtypes.bfloat16)}
    for k in ("g_mix", "w_in", "g_q", "g_k", "lam_q1", "lam_k1", "lam_q2", "lam_k2", "g_sub", "w_branch_a",
              "w_branch_b", "w_gate", "b_gate", "w_out", "g_ffn", "w_ffn_gate", "w_ffn_up", "w_ffn_down"):
        m[k] = np.ascontiguousarray(np.asarray(inputs[k], dtype=np.float32))
    return m


def kernel(**inputs):
    nc = build_nc()
    in_maps = [_core_inputs(inputs, core // 2, core % 2) for core in range(8)]
    res = run_bass_kernel_spmd(nc, in_maps, core_ids=list(range(8)))
    outf = np.zeros((BATCH, SEQ, D), np.float32)
    for core in range(8):
        b, c = core // 2, core % 2
        o = np.asarray(res.results[core]["out"], dtype=np.float32)
        for i in range(4):
            sbi = 2 * i + c
            outf[b, sbi * SBT:(sbi + 1) * SBT] = o[i * SBT:(i + 1) * SBT]
    return outf
```

```python
import math
from contextlib import ExitStack

import numpy as np
import ml_dtypes
import concourse.bass as bass
import concourse.mybir as mybir
from concourse.bass_utils import run_bass_kernel_spmd

F32 = mybir.dt.float32
BF16 = mybir.dt.bfloat16
AF = mybir.ActivationFunctionType
ALU = mybir.AluOpType
AX = mybir.AxisListType.X

D = 1024
SEQ = 4096
BATCH = 4
SBT = 512
NOWN = 2048
DFF = 2816
NFC = DFF // 128
EPS = 1e-6
LAMBDA_INIT = 0.8 - 0.6 * math.exp(-0.3 * 0)
NEG = -30000.0
N_DUMMY = 0
N_DUMMY_PROJ = 0


class Sched:
    ENGS = ("pe", "act", "dve", "pool", "sp")

    def __init__(self, nc, stack, n_dma_sems=24):
        self.nc = nc
        self.ops = {e: [] for e in self.ENGS}
        self.cnt = {e: 0 for e in self.ENGS}
        self.seen = {e: {} for e in self.ENGS}
        self.res = {}
        self.sem = {}
        for e in self.ENGS:
            self.sem[e] = stack.enter_context(nc.semaphore("s_" + e))
        self.ndma = n_dma_sems
        for i in range(n_dma_sems):
            self.sem[("d", i)] = stack.enter_context(nc.semaphore("s_d%d" % i))
        self.dval = [0] * n_dma_sems
        self.dnext = 0
        self.n_sw = 0
        self.stack = stack
        self.pending_nosig = {e: False for e in self.ENGS}

    def _r(self, key):
        if key not in self.res:
            self.res[key] = {"w": None, "r": {}}
        return self.res[key]

    def _wait(self, eng, tok):
        if tok is None:
            return
        k, v = tok
        if k == eng == "pe":
            return
        if self.seen[eng].get(k, 0) >= v:
            return
        self.seen[eng][k] = v
        sem = self.sem[k]
        self.ops[eng].append(lambda e, sem=sem, v=v: e.wait_ge(sem, v))

    def _deps(self, eng, reads, writes):
        for r in reads:
            self._wait(eng, self._r(r)["w"])
        for w in writes:
            rr = self._r(w)
            self._wait(eng, rr["w"])
            for k, v in list(rr["r"].items()):
                self._wait(eng, (k, v))

    def _update(self, tok, reads, writes):
        k, v = tok
        for r in reads:
            rr = self._r(r)
            rr["r"][k] = max(rr["r"].get(k, 0), v)
        for w in writes:
            self.res[w] = {"w": tok, "r": {}}

    def op(self, eng, fn, reads=(), writes=(), signal=True):
        self._deps(eng, reads, writes)
        if signal:
            self.cnt[eng] += 1
            tok = (eng, self.cnt[eng])
            sem = self.sem[eng]
            self.ops[eng].append(lambda e, fn=fn, sem=sem: fn(e).then_inc(sem, 1))
            self.pending_nosig[eng] = False
        else:
            tok = (eng, self.cnt[eng] + 1)
            self.ops[eng].append(lambda e, fn=fn: fn(e))
            self.pending_nosig[eng] = True
        self._update(tok, reads, writes)
        return tok

    def group(self, eng, fns, reads=(), writes=()):
        for f in fns[:-1]:
            self.op(eng, f, reads, writes, signal=False)
        return self.op(eng, fns[-1], reads, writes, signal=True)

    def dma(self, out, in_, reads=(), writes=(), q="sp", **kw):
        if q == "pool":
            k = ("w", self.n_sw)
            self.sem[k] = self.stack.enter_context(self.nc.semaphore("s_w%d" % self.n_sw))
            self.n_sw += 1
            self._deps(q, reads, writes)
            tok = (k, 16)
            self.sw_toks = getattr(self, "sw_toks", []) + [tok]
        else:
            i = self.dnext
            self.dnext = (self.dnext + 1) % self.ndma
            k = ("d", i)
            if self.dval[i] > 0:
                self._wait(q, (k, self.dval[i]))
            self._deps(q, reads, writes)
            self.dval[i] += 16
            tok = (k, self.dval[i])
        sem = self.sem[k]
        self.ops[q].append(lambda e, out=out, in_=in_, sem=sem, kw=kw: e.dma_start(out=out, in_=in_, **kw).then_inc(sem, 16))
        self._update(tok, reads, writes)
        return tok

    def fence(self):
        for e in self.ENGS:
            for x in self.ENGS:
                if x != e and self.cnt[x] > 0:
                    self._wait(e, (x, self.cnt[x]))
            for i in range(self.ndma):
                if self.dval[i] > 0:
                    self._wait(e, (("d", i), self.dval[i]))
            for tok in getattr(self, "sw_toks", []):
                self._wait(e, tok)

    def finish(self, eng="sp"):
        for i in range(self.ndma):
            if self.dval[i] > 0:
                self._wait(eng, (("d", i), self.dval[i]))
        for e in self.ENGS:
            assert not self.pending_nosig[e], e
        nc = self.nc
        with nc.Block() as block:
            @block.tensor
            def _(e):
                for f in self.ops["pe"]:
                    f(e)

            @block.scalar
            def _(e):
                for f in self.ops["act"]:
                    f(e)

            @block.vector
            def _(e):
                for f in self.ops["dve"]:
                    f(e)

            @block.gpsimd
            def _(e):
                for f in self.ops["pool"]:
                    f(e)

            @block.sync
            def _(e):
                for f in self.ops["sp"]:
                    f(e)


_UID = [0]


def _uid(name):
    _UID[0] += 1
    return "%s_u%d" % (name, _UID[0])


class Ring:
    def __init__(self, nc, stack, name, shape, dt, n, psum=False):
        alloc = nc.psum_tensor if psum else nc.sbuf_tensor
        self.t = [stack.enter_context(alloc(_uid("rg_%s%d" % (name, i)), shape, dt)) for i in range(n)]
        self.k = ["%s%d" % (name, i) for i in range(n)]
        self.i = 0

    def next(self):
        i = self.i
        self.i = (i + 1) % len(self.t)
        return self.t[i], self.k[i]


class ViewRing:
    def __init__(self, name, views):
        self.t = views
        self.k = ["%s%d" % (name, i) for i in range(len(views))]
        self.i = 0

    def next(self):
        i = self.i
        self.i = (i + 1) % len(self.t)
        return self.t[i], self.k[i]


def MM(out, lhsT, rhs, start=True, stop=True):
    return lambda e: e.matmul(out, lhsT=lhsT, rhs=rhs, start=start, stop=stop, skip_group_check=True)


def TR(out, in_, ident):
    return lambda e: e.transpose(out=out, in_=in_, identity=ident)


def ACT(out, in_, func, **kw):
    return lambda e: e.activation(out=out, in_=in_, func=func, **kw)


def TT(out, a, b, op):
    return lambda e: e.tensor_tensor(out=out, in0=a, in1=b, op=op)


def TS(out, a, s1, s2, op0, op1=None):
    if op1 is None:
        return lambda e: e.tensor_scalar(out=out, in0=a, scalar1=s1, scalar2=None, op0=op0)
    return lambda e: e.tensor_scalar(out=out, in0=a, scalar1=s1, scalar2=s2, op0=op0, op1=op1)


def STT(out, a, s, b, op0, op1):
    return lambda e: e.scalar_tensor_tensor(out=out, in0=a, scalar=s, in1=b, op0=op0, op1=op1)


def CP(out, in_):
    return lambda e: e.tensor_copy(out=out, in_=in_)


def RED(out, in_):
    return lambda e: e.tensor_reduce(out=out, in_=in_, axis=AX, op=ALU.add)


def MS(ap, v):
    return lambda e: e.memset(ap, v)


def build_nc(dbg=None):
    nc = bass.Bass("TRN2", target_bir_lowering=False)

    def din(name, shape):
        return nc.dram_tensor(name, shape, F32, kind="ExternalInput").ap()

    xs = din("xs", [SEQ, D])
    cs_d = din("cs", [128, 32 * 32])
    bg_d = din("bgl", [128, 16])
    gqk_d = din("gqk", [128, 2 * 512])
    lamb_d = din("lamb", [128, 4 * 64])
    vones_d = nc.dram_tensor("vones", [128, 128], BF16, kind="ExternalInput").ap()
    consts_d = nc.dram_tensor("consts", [128, 640], BF16, kind="ExternalInput").ap()
    g_mix = din("g_mix", [1, D])
    w_in = din("w_in", [1, D, 3072])
    g_q = din("g_q", [1, 64])
    g_k = din("g_k", [1, 64])
    lam_q1 = din("lam_q1", [1, 64])
    lam_k1 = din("lam_k1", [1, 64])
    lam_q2 = din("lam_q2", [1, 64])
    lam_k2 = din("lam_k2", [1, 64])
    g_sub = din("g_sub", [1, 128])
    w_ba = din("w_branch_a", [1, 512, D])
    w_bb = din("w_branch_b", [1, 512, D])
    w_gate = din("w_gate", [1, D, 2 * D])
    b_gate = din("b_gate", [1, 2 * D])
    w_out = din("w_out", [1, D, D])
    g_ffn = din("g_ffn", [1, D])
    w_fg = din("w_ffn_gate", [1, D, DFF])
    w_fu = din("w_ffn_up", [1, D, DFF])
    w_fd = din("w_ffn_down", [1, DFF, D])
    out = nc.dram_tensor("out", [NOWN, D], F32, kind="ExternalOutput").ap()
    dbg_t = None
    if dbg in ("da", "sb"):
        dbg_t = nc.dram_tensor("dbg", [128, 4, NOWN], BF16, kind="ExternalOutput").ap()

    with ExitStack() as st:
        S = Sched(nc, st)

        def sbt(stack, name, shape, dt):
            return stack.enter_context(nc.sbuf_tensor(_uid("sb_" + name), shape, dt))

        def pst(stack, name, shape, dt):
            return stack.enter_context(nc.psum_tensor(_uid("pm_" + name), shape, dt))

        cst = sbt(st, "cst", [128, 640], BF16)
        ident = cst[:, 0:128]
        trimask = cst[:, 128:256]
        negtri = cst[:, 256:384]
        negones = cst[:, 384:512]
        ones = cst[:, 512:640]
        vones = sbt(st, "vones", [128, 128], BF16)
        gffn_bc = sbt(st, "gffn_bc", [128, D], F32)
        bgate = sbt(st, "bgate", [128, 16], F32)
        S.dma(cst[:], consts_d, writes=["cst"])
        S.dma(vones[:], vones_d, writes=["vones"])
        S.dma(gffn_bc[:], g_ffn[0, :].partition_broadcast(128), writes=["gffn_bc"])
        S.dma(bgate[:], bg_d, writes=["bgate"], q="act")

        s_att = ExitStack()
        st.enter_context(s_att)
        pers = sbt(s_att, "pers", [128, 8, NOWN], BF16)
        oaT = pers[:, 0:4, :]
        obT = pers[:, 4:8, :]
        scr = pers[:].rearrange("p a b -> p (a b)").bitcast(F32)
        gmix_bc = sbt(s_att, "gmix_bc", [128, D], F32)
        S.dma(gmix_bc[:], g_mix[0, :].partition_broadcast(128), writes=["gmix_bc"])

        def rms_to_hT(R, xt, xk, gbc, gk, hT, hk, tcol, par):
            junk, jk = R["junk"].next()
            ss, sk = R["ss"].next()
            hb, hbk = R["hb"].next()
            pT, pk = R["pT"].next()
            S.op("act", ACT(junk[:], xt, AF.Square, accum_out=ss[:, 0:1]), reads=[xk], writes=[jk, sk])
            S.op("act", ACT(ss[:, 1:2], ss[:, 0:1], AF.Ln, scale=1.0 / D, bias=EPS), reads=[sk], writes=[sk])
            S.op("act", ACT(ss[:, 2:3], ss[:, 1:2], AF.Exp, scale=-0.5), reads=[sk], writes=[sk])
            S.op("dve", STT(hb[:], xt, ss[:, 2:3], gbc[:], ALU.mult, ALU.mult), reads=[xk, sk, gk], writes=[hbk])
            S.group("pe", [TR(pT[:, dc, :], hb[:, dc * 128:(dc + 1) * 128], ident) for dc in range(8)],
                    reads=[hbk, "cst"], writes=[pk])
            eng = "act" if par % 2 == 0 else "dve"
            if eng == "act":
                S.op("act", ACT(hT[:, :, tcol:tcol + 128], pT[:], AF.Copy), reads=[pk], writes=[hk])
            else:
                S.op("dve", CP(hT[:, :, tcol:tcol + 128], pT[:]), reads=[pk], writes=[hk])

        def run_pipeline(jobs):
            active = []
            it = iter(jobs)
            more = True
            while more or active:
                nxt = []
                for g in active:
                    try:
                        next(g)
                        nxt.append(g)
                    except StopIteration:
                        pass
                active = nxt
                if more:
                    try:
                        g = next(it)
                    except StopIteration:
                        more = False
                        continue
                    try:
                        next(g)
                        active.append(g)
                    except StopIteration:
                        pass

        def rms_stat(R, xt, xk):
            junk, jk = R["junk"].next()
            ss, sk = R["ss"].next()
            S.op("act", ACT(junk[:], xt, AF.Square, accum_out=ss[:, 0:1]), reads=[xk], writes=[jk, sk])
            S.op("act", ACT(ss[:, 1:2], ss[:, 0:1], AF.Ln, scale=1.0 / D, bias=EPS), reads=[sk], writes=[sk])
            S.op("act", ACT(ss[:, 2:3], ss[:, 1:2], AF.Exp, scale=-0.5), reads=[sk], writes=[sk])
            return ss, sk

        def rms_apply(R, xt, xk, ss, sk, gbc, gk):
            hb, hbk = R["hb"].next()
            S.op("dve", STT(hb[:], xt, ss[:, 2:3], gbc[:], ALU.mult, ALU.mult), reads=[xk, sk, gk], writes=[hbk])
            return hb, hbk

        def rms_stage(R, xt, xk, gbc, gk):
            ss, sk = rms_stat(R, xt, xk)
            return rms_apply(R, xt, xk, ss, sk, gbc, gk)

        def tr_stage(R, hb, hbk, dst, dk, par):
            pT, pk = R["pT"].next()
            S.group("pe", [TR(pT[:, dc, :], hb[:, dc * 128:(dc + 1) * 128], ident) for dc in range(8)],
                    reads=[hbk, "cst"], writes=[pk])
            if par % 2 == 0:
                S.op("act", ACT(dst, pT[:], AF.Copy), reads=[pk], writes=[dk])
            else:
                S.op("dve", CP(dst, pT[:]), reads=[pk], writes=[dk])

        def att_units(j):
            units = [(j, kb, True, False) for kb in (3, 2, 1, 0)]
            for i in range(j, -1, -1):
                units += [(4 + i, kb, False, i == 0) for kb in (3, 2, 1, 0)]
                if i > 0:
                    units += [(i - 1, kb, False, False) for kb in (3, 2, 1, 0)]
            return units

        with ExitStack() as sda:
            gq_bc = sbt(sda, "gq_bc", [128, 8, 64], F32)
            gk_bc = sbt(sda, "gk_bc", [128, 8, 64], F32)
            cs_all = sbt(sda, "cs_all", [128, 32, 32], F32)
            lamv = sbt(sda, "lamv", [128, 4, 64], F32)
            lamt = sbt(sda, "lamt", [128, 2, 64], F32)
            lams = sbt(sda, "lams", [128, 2], F32)
            lame = sbt(sda, "lame", [128, 2], F32)
            neglam = sbt(sda, "neglam", [128, 1], F32)
            gsub8 = sbt(sda, "gsub8", [128, 1], F32)
            S.dma(gq_bc[:], gqk_d[:, 0:512].rearrange("p (s d) -> p s d", d=64), writes=[("gq_ld", 0)], q="act")
            S.dma(gk_bc[:], gqk_d[:, 512:1024].rearrange("p (s d) -> p s d", d=64), writes=[("gk_ld", 0)], q="act")
            S.dma(cs_all[:], cs_d.rearrange("p (b c) -> p b c", c=32), writes=["cs_all"], q="act")
            S.dma(lamv[:], lamb_d.rearrange("p (i d) -> p i d", d=64), writes=[("lamv", 0)], q="act")
            S.dma(gsub8[:], g_sub[0, :].rearrange("(p o) -> p o", o=1), writes=["gsub8"], q="act")

            def small_param_math():
                lk = [("lamv", 0)]
                S.op("dve", TT(lamt[:, 0, :], lamv[:, 0, :], lamv[:, 1, :], ALU.mult), reads=lk, writes=["lamt"])
                S.op("dve", TT(lamt[:, 1, :], lamv[:, 2, :], lamv[:, 3, :], ALU.mult), reads=lk, writes=["lamt"])
                S.op("dve", RED(lams[:], lamt[:]), reads=["lamt"], writes=["lams"])
                S.op("act", ACT(lame[:], lams[:], AF.Exp), reads=["lams"], writes=["lame"])
                S.op("dve", TT(neglam[:], lame[:, 1:2], lame[:, 0:1], ALU.subtract), reads=["lame"], writes=["neglam"])
                S.op("dve", TS(neglam[:], neglam[:], -LAMBDA_INIT, None, ALU.add), reads=["neglam"], writes=["neglam"])
                S.op("dve", TS(gsub8[:], gsub8[:], 1.0 - LAMBDA_INIT, None, ALU.mult), reads=["gsub8"], writes=["gsub8"])
            KT = sbt(sda, "KT_da", [128, 4, SEQ], BF16)
            QT = sbt(sda, "QT_da", [128, 4, NOWN], BF16)
            Vd = sbt(sda, "V_da", [128, 32, 512], BF16)
            with ExitStack() as sp1:
                Wq = sbt(sp1, "Wdq", [128, 8, 512], BF16)
                Wk = sbt(sp1, "Wdk", [128, 8, 512], BF16)
                Wv = sbt(sp1, "Wdv", [128, 8, 512], BF16)
                S.dma(Wk[:], w_in[0, :, 2048:2560].rearrange("(k p) n -> p k n", p=128), writes=["Wdk"], q="pool")
                S.dma(Wv[:], w_in[0, :, 2560:3072].rearrange("(k p) n -> p k n", p=128), writes=["Wdv"], q="pool")
                S.dma(Wq[:], w_in[0, :, 1536:2048].rearrange("(k p) n -> p k n", p=128), writes=["Wdq"], q="pool")
                S.op("dve", TS(gq_bc[:], gq_bc[:], 0.125, None, ALU.mult), reads=[("gq_ld", 0)], writes=["gq_bc"])
                S.op("dve", TS(gk_bc[:], gk_bc[:], 1.0, None, ALU.mult), reads=[("gk_ld", 0)], writes=["gk_bc"])
                R = {
                    "x": ViewRing("xtv", [scr[:, i * 1024:(i + 1) * 1024] for i in range(4)]),
                    "junk": Ring(nc, sp1, "junk", [128, D], BF16, 1),
                    "ss": Ring(nc, sp1, "ss", [128, 4], F32, 4),
                    "hb": Ring(nc, sp1, "hb", [128, D], BF16, 3),
                    "pT": Ring(nc, sp1, "pT", [128, 8, 128], BF16, 2, psum=True),
                    "hTt": Ring(nc, sp1, "hTt", [128, 8, 128], BF16, 3),
                    "pp": Ring(nc, sp1, "pp", [128, 512], F32, 4, psum=True),
                    "pT2": Ring(nc, sp1, "pTq", [128, 8, 128], BF16, 2, psum=True),
                    "f": ViewRing("qkfv", [scr[:, 4096 + i * 512:4096 + (i + 1) * 512].rearrange("p (s d) -> p s d", d=64) for i in range(8)]),
                    "sq": Ring(nc, sp1, "qksq", [128, 8, 64], F32, 4),
                    "n1": Ring(nc, sp1, "qkn1", [128, 8, 64], F32, 3),
                    "xg": Ring(nc, sp1, "qkxg", [128, 8, 16], F32, 4),
                    "s8": Ring(nc, sp1, "s8", [128, 3, 8], F32, 8),
                    "rt": Ring(nc, sp1, "rt", [128, 2, 8, 16], F32, 3),
                    "b": Ring(nc, sp1, "qkb", [128, 512], BF16, 6),
                }

                def da_proj_job(blk):
                    n, t = divmod(blk, 4)
                    xt, xk = R["x"].next()
                    S.dma(xt, xs[blk * 128:(blk + 1) * 128, :], writes=[xk])
                    yield
                    ss, sk = rms_stat(R, xt, xk)
                    yield
                    hb, hbk = rms_apply(R, xt, xk, ss, sk, gmix_bc, "gmix_bc")
                    yield
                    hTt, hk = R["hTt"].next()
                    tr_stage(R, hb, hbk, hTt[:], hk, 0)
                    yield
                    lhs = [hTt[:, dc, :] for dc in range(8)]
                    paths = []
                    pp, ppk = R["pp"].next()
                    S.group("pe", [MM(pp[:], lhs[dc], Wk[:, dc, :], start=(dc == 0), stop=(dc == 7)) for dc in range(8)],
                            reads=[hk, "Wdk"], writes=[ppk])
                    f, fk = R["f"].next()
                    S.op("act", ACT(f.rearrange("p s d -> p (s d)"), pp[:], AF.Copy), reads=[ppk], writes=[fk])
                    sq, sqk = R["sq"].next()
                    S.op("act", ACT(sq[:], f, AF.Square), reads=[fk], writes=[sqk])
                    paths.append((f, fk, sq, sqk, gk_bc, "gk_bc", KT, ("KT", blk)))
                    pp, ppk = R["pp"].next()
                    S.group("pe", [MM(pp[:], lhs[dc], Wv[:, dc, :], start=(dc == 0), stop=(dc == 7)) for dc in range(8)],
                            reads=[hk, "Wdv"], writes=[ppk])
                    S.op("act", ACT(Vd[:, blk, :], pp[:], AF.Copy), reads=[ppk], writes=[("Vd", blk)])
                    if n < 4:
                        pp, ppk = R["pp"].next()
                        S.group("pe", [MM(pp[:], lhs[dc], Wq[:, dc, :], start=(dc == 0), stop=(dc == 7)) for dc in range(8)],
                                reads=[hk, "Wdq"], writes=[ppk])
                        f, fk = R["f"].next()
                        S.op("act", ACT(f.rearrange("p s d -> p (s d)"), pp[:], AF.Copy), reads=[ppk], writes=[fk])
                        sq, sqk = R["sq"].next()
                        S.op("act", ACT(sq[:], f, AF.Square), reads=[fk], writes=[sqk])
                        paths.append((f, fk, sq, sqk, gq_bc, "gq_bc", QT, ("QT", blk)))
                    yield
                    st1 = []
                    for (f, fk, sq, sqk, gbc, gkey, dstT, dkey) in paths:
                        s8, s8k = R["s8"].next()
                        S.op("dve", RED(s8[:, 0, :], sq[:]), reads=[sqk], writes=[s8k])
                        st1.append((f, fk, gbc, gkey, dstT, dkey, s8, s8k))
                    yield
                    for (f, fk, gbc, gkey, dstT, dkey, s8, s8k) in st1:
                        S.op("act", ACT(s8[:, 1, :], s8[:, 0, :], AF.Ln, scale=1.0 / 64, bias=EPS), reads=[s8k], writes=[s8k])
                        S.op("act", ACT(s8[:, 2, :], s8[:, 1, :], AF.Exp, scale=-0.5), reads=[s8k], writes=[s8k])
                    yield
                    st2 = []
                    for (f, fk, gbc, gkey, dstT, dkey, s8, s8k) in st1:
                        n1, n1k = R["n1"].next()
                        b, bk = R["b"].next()
                        xg, xgk = R["xg"].next()
                        bv = b[:].rearrange("p (s d) -> p s d", d=64)
                        S.op("dve", TT(n1[:], f, s8[:, 2, :].to_broadcast([128, 8, 64]), ALU.mult), reads=[fk, s8k], writes=[n1k])
                        S.op("dve", TT(bv, n1[:], gbc[:], ALU.mult), reads=[n1k, gkey], writes=[bk])
                        S.op("dve", TT(xg[:], n1[:, :, 0:16], gbc[:, :, 0:16], ALU.mult), reads=[n1k, gkey], writes=[xgk])
                        st2.append((xg, xgk, b, bk, bv, dstT, dkey))
                    yield
                    st3 = []
                    for (xg, xgk, b, bk, bv, dstT, dkey) in st2:
                        rt, rtk = R["rt"].next()
                        ccb = cs_all[:, blk, 0:16].unsqueeze(1).to_broadcast([128, 8, 16])
                        ssb_ = cs_all[:, blk, 16:32].unsqueeze(1).to_broadcast([128, 8, 16])
                        S.op("dve", TT(rt[:, 0], xg[:], ccb, ALU.mult), reads=[xgk, "cs_all"], writes=[rtk])
                        S.op("dve", TT(rt[:, 1], xg[:], ssb_, ALU.mult), reads=[xgk, "cs_all"], writes=[rtk])
                        S.op("dve", TT(bv[:, :, 0:8], rt[:, 0, :, 0:8], rt[:, 1, :, 8:16], ALU.subtract), reads=[rtk], writes=[bk])
                        S.op("dve", TT(bv[:, :, 8:16], rt[:, 0, :, 8:16], rt[:, 1, :, 0:8], ALU.add), reads=[rtk], writes=[bk])
                        st3.append((b, bk, dstT, dkey))
                    yield
                    for pi, (b, bk, dstT, dkey) in enumerate(st3):
                        pT2, pT2k = R["pT2"].next()
                        S.group("pe", [TR(pT2[:, hh, :], b[:, hh * 128:(hh + 1) * 128], ident) for hh in range(4)],
                                reads=[bk, "cst"], writes=[pT2k])
                        dcol = blk * 128
                        S.op("act", ACT(dstT[:, :, dcol:dcol + 128], pT2[:, 0:4, :], AF.Copy), reads=[pT2k], writes=[dkey])

                run_pipeline([da_proj_job(blk) for blk in range(32)])

            small_param_math()
            S.fence()
            with ExitStack() as sp2:
                PS2 = Ring(nc, sp2, "ps_s", [128, 2, 512], F32, 2, psum=True)
                o1 = pst(sp2, "ps_o1", [128, 512], F32)
                o2 = pst(sp2, "ps_o2", [128, 512], F32)
                lacc = pst(sp2, "ps_lacc", [128, 512], F32)
                sel = sbt(sp2, "sel", [128, 2, 128], BF16)
                rhl = sbt(sp2, "rhl", [128, 2, 512], BF16)
                S.op("pool", MS(sel[:], 0.0), writes=["sel"])
                S.op("pool", MS(sel[0:1, 0, :], 1.0), writes=["sel"])
                S.op("pool", MS(sel[64:65, 1, :], 1.0), writes=["sel"])
                PB = Ring(nc, sp2, "Pb", [128, 2, 512], BF16, 4)
                EP = {k: sbt(sp2, "ep_" + k, [128, 512], F32) for k in ("t1", "r1", "t2", "r2", "c1", "c2", "a1", "a2", "ob", "ln", "rs")}
                sqb = sbt(sp2, "ep_sq", [128, 512], BF16)

                da_pend = []
                da_later = []
                epb = pst(sp2, "ps_ep", [128, 512], F32)

                def da_att_job(h, j, u, nun, nst, kb, diag, usev):
                    blk = nst * 4 + kb
                    c0 = kb * 128 if diag else 0
                    q0 = j * SBT + c0
                    q1 = (j + 1) * SBT
                    sp_, spk = PS2.next()
                    qkeys = [("QT", j * 4 + tt) for tt in range(4)]
                    S.group("pe", [MM(sp_[:, 0, c0:], KT[0:64, h, blk * 128:(blk + 1) * 128], QT[0:64, h, q0:q1]),
                                   MM(sp_[:, 1, c0:], KT[64:128, h, blk * 128:(blk + 1) * 128], QT[64:128, h, q0:q1])],
                            reads=[("KT", blk)] + qkeys, writes=[spk])
                    pb, pbk = PB.next()
                    S.op("act", ACT(pb[:, :, c0:], sp_[:, :, c0:], AF.Exp), reads=[spk], writes=[pbk])
                    if diag:
                        S.op("pool", MS(pb[64:128, :, c0:c0 + 64], 0.0), reads=[], writes=[pbk])

                    def emit_pv(h=h, j=j, u=u, nun=nun, blk=blk, c0=c0, usev=usev, pb=pb, pbk=pbk):
                        _emit_pv(h, j, u, nun, blk, c0, usev, pb, pbk)

                    da_pend.append(emit_pv)
                    if len(da_pend) > 2:
                        da_pend.pop(0)()
                    da_tick()
                    return
                    yield

                def _emit_pv(h, j, u, nun, blk, c0, usev, pb, pbk):
                    lo = vones[:] if usev else ones
                    lok = "vones" if usev else "cst"
                    first = (u == 0)
                    last = (u == nun - 1)
                    vv = Vd[:, blk, h * 128:(h + 1) * 128]
                    S.group("pe", [MM(o1[:, c0:], vv, pb[:, 0, c0:], start=first, stop=last),
                                   MM(o2[:, c0:], vv, pb[:, 1, c0:], start=first, stop=last),
                                   MM(lacc[0:64, c0:], lo[:, 0:64], pb[:, 0, c0:], start=first, stop=last),
                                   MM(lacc[64:128, c0:], lo[:, 0:64], pb[:, 1, c0:], start=first, stop=last)],
                            reads=[("Vd", blk), pbk, lok], writes=["o1", "lacc", "o2"])
                    if not last:
                        return
                    E = EP

                    def e1():
                        S.op("act", ACT(E["t1"][:], lacc[:], AF.Ln), reads=["lacc"], writes=["e_t1"])
                        S.op("dve", CP(E["c1"][:], o1[:]), reads=["o1"], writes=["e_c1"])
                        S.op("dve", CP(E["c2"][:], o2[:]), reads=["o2"], writes=["e_c2"])
                        S.op("act", ACT(E["r1"][:], E["t1"][:], AF.Exp, scale=-1.0), reads=["e_t1"], writes=["e_r1"])
                        S.op("dve", CP(rhl[:, 0, :], E["r1"][:]), reads=["e_r1"], writes=["rhl"])
                        S.op("dve", TT(rhl[:, 1, :], E["r1"][:], rhl[:, 0, :], ALU.subtract), reads=["e_r1", "rhl"], writes=["rhl"])

                    def e2():
                        S.group("pe", [MM(epb[:], sel[:, 0, :], rhl[:, 0, :], start=True, stop=False),
                                       MM(epb[:], sel[:, 0, :], rhl[:, 1, :], start=False, stop=True)],
                                reads=["sel", "rhl"], writes=["epb"])
                        S.op("dve", TT(E["a1"][:], E["c1"][:], epb[:], ALU.mult), reads=["e_c1", "epb"], writes=["e_a1"])
                        S.group("pe", [MM(epb[:], sel[:, 1, :], rhl[:, 0, :], start=True, stop=False),
                                       MM(epb[:], sel[:, 1, :], rhl[:, 1, :], start=False, stop=True)],
                                reads=["sel", "rhl"], writes=["epb"])
                        S.op("dve", STT(E["a2"][:], E["c2"][:], neglam[:, 0:1], epb[:], ALU.mult, ALU.mult),
                             reads=["e_c2", "epb", "neglam"], writes=["e_a2"])
                        S.op("dve", TT(E["ob"][:], E["a1"][:], E["a2"][:], ALU.add), reads=["e_a1", "e_a2"], writes=["e_ob"])
                        S.op("act", ACT(sqb[:], E["ob"][:], AF.Square), reads=["e_ob"], writes=["e_sq"])

                    def e3(h=h, j=j):
                        S.op("pe", MM(epb[:], ones, sqb[:]), reads=["e_sq", "cst"], writes=["epb"])
                        S.op("act", ACT(E["ln"][:], epb[:], AF.Ln, scale=1.0 / 128, bias=EPS), reads=["epb"], writes=["e_ln"])
                        S.op("act", ACT(E["rs"][:], E["ln"][:], AF.Exp, scale=-0.5), reads=["e_ln"], writes=["e_rs"])
                        S.op("dve", STT(obT[:, h, j * SBT:(j + 1) * SBT], E["ob"][:], gsub8[:, 0:1], E["rs"][:], ALU.mult, ALU.mult),
                             reads=["e_ob", "e_rs", "gsub8"], writes=[("obT", h, j)])

                    e1()
                    da_later.append([2, e2])
                    da_later.append([5, e3])

                def da_tick():
                    for item in list(da_later):
                        item[0] -= 1
                        if item[0] <= 0:
                            da_later.remove(item)
                            item[1]()

                jobs = []
                for h in range(4):
                    for j in range(4):
                        units = att_units(j)
                        for u, (nst, kb, diag, usev) in enumerate(units):
                            jobs.append(da_att_job(h, j, u, len(units), nst, kb, diag, usev))
                run_pipeline(jobs)
                while da_pend:
                    da_pend.pop(0)()
                    da_tick()
                while da_later:
                    da_tick()

        S.fence()
        if dbg == "da":
            S.dma(dbg_t, obT, reads=[("obT", h, j) for h in range(4) for j in range(4)])
            S.finish()
            return nc

        s_sbo = ExitStack()
        s_att.enter_context(s_sbo)
        hTown = [sbt(s_sbo, "hTown%d" % n, [128, 8, SBT], BF16) for n in range(4)]
        with ExitStack() as ssb:
            KTs = sbt(ssb, "KT_sb", [128, 4, SEQ], BF16)
            QTs = sbt(ssb, "QT_sb", [128, 4, NOWN], BF16)
            Vs = sbt(ssb, "V_sb", [128, 32, 512], BF16)
            with ExitStack() as sp1:
                Wq = sbt(sp1, "Wsq", [128, 8, 512], BF16)
                Wk = sbt(sp1, "Wsk", [128, 8, 512], BF16)
                Wv = sbt(sp1, "Wsv", [128, 8, 512], BF16)
                S.dma(Wk[:], w_in[0, :, 512:1024].rearrange("(k p) n -> p k n", p=128), writes=["Wsk"], q="pool")
                S.dma(Wq[:], w_in[0, :, 0:512].rearrange("(k p) n -> p k n", p=128), writes=["Wsq"], q="pool")
                S.dma(Wv[:], w_in[0, :, 1024:1536].rearrange("(k p) n -> p k n", p=128), writes=["Wsv"], q="pool")
                R = {
                    "x": ViewRing("xtv2", [scr[:, i * 1024:(i + 1) * 1024] for i in range(4)]),
                    "junk": Ring(nc, sp1, "junk", [128, D], BF16, 1),
                    "ss": Ring(nc, sp1, "ss", [128, 4], F32, 4),
                    "hb": Ring(nc, sp1, "hb", [128, D], BF16, 3),
                    "pT": Ring(nc, sp1, "pT", [128, 8, 128], BF16, 2, psum=True),
                    "hT": Ring(nc, sp1, "hT", [128, 8, SBT], BF16, 2),
                    "pp": Ring(nc, sp1, "pp", [128, 512], F32, 4, psum=True),
                }
                sbh = {}
                dmy_sp = pst(sp1, "ps_dmysp", [128, 512], F32)

                def sb_proj_job(blk):
                    n, t = divmod(blk, 4)
                    if t == 0:
                        sbh[n] = (hTown[n], "hTown%d" % n) if n < 4 else R["hT"].next()
                    xt, xk = R["x"].next()
                    S.dma(xt, xs[blk * 128:(blk + 1) * 128, :], writes=[xk])
                    yield
                    ss, sk = rms_stat(R, xt, xk)
                    yield
                    hb, hbk = rms_apply(R, xt, xk, ss, sk, gmix_bc, "gmix_bc")
                    yield
                    hT, hk = sbh[n]
                    for _ in range(N_DUMMY_PROJ):
                        S.op("pe", MM(dmy_sp[:], ident, Wk[:, 0, :]), reads=["cst", "Wsk"], writes=[], signal=False)
                    tr_stage(R, hb, hbk, hT[:, :, t * 128:(t + 1) * 128], hk, blk)
                    if t != 3:
                        return
                    yield
                    par = 0
                    for p in range(4):
                        if p == 2:
                            yield
                        pp, ppk = R["pp"].next()
                        S.group("pe", [MM(pp[:], Wk[:, dc, p * 128:(p + 1) * 128], hT[:, dc, :], start=(dc == 0), stop=(dc == 7))
                                       for dc in range(8)], reads=[hk, "Wsk"], writes=[ppk])
                        par += 1
                        if par % 2 == 0:
                            S.op("act", ACT(KTs[:, p, n * SBT:(n + 1) * SBT], pp[:], AF.Copy), reads=[ppk], writes=[("KTs", n, p)])
                        else:
                            S.op("dve", CP(KTs[:, p, n * SBT:(n + 1) * SBT], pp[:]), reads=[ppk], writes=[("KTs", n, p)])
                        if n < 4:
                            pp, ppk = R["pp"].next()
                            S.group("pe", [MM(pp[:], Wq[:, dc, p * 128:(p + 1) * 128], hT[:, dc, :], start=(dc == 0), stop=(dc == 7))
                                           for dc in range(8)], reads=[hk, "Wsq"], writes=[ppk])
                            par += 1
                            if par % 2 == 0:
                                S.op("act", ACT(QTs[:, p, n * SBT:(n + 1) * SBT], pp[:], AF.Copy, scale=0.125),
                                     reads=[ppk], writes=[("QTs", n, p)])
                            else:
                                S.op("dve", TS(QTs[:, p, n * SBT:(n + 1) * SBT], pp[:], 0.125, None, ALU.mult),
                                     reads=[ppk], writes=[("QTs", n, p)])
                    for tt in range(4):
                        if tt % 2 == 0:
                            yield
                        bb = n * 4 + tt
                        pp, ppk = R["pp"].next()
                        S.group("pe", [MM(pp[:], hT[:, dc, tt * 128:(tt + 1) * 128], Wv[:, dc, :], start=(dc == 0), stop=(dc == 7))
                                       for dc in range(8)], reads=[hk, "Wsv"], writes=[ppk])
                        par += 1
                        if par % 2 == 0:
                            S.op("act", ACT(Vs[:, bb, :], pp[:], AF.Copy), reads=[ppk], writes=[("Vs", bb)])
                        else:
                            S.op("dve", CP(Vs[:, bb, :], pp[:]), reads=[ppk], writes=[("Vs", bb)])

                run_pipeline([sb_proj_job(blk) for blk in range(32)])

            S.fence()
            with ExitStack() as sp2:
                ZR = Ring(nc, sp2, "ps_z", [128, 2, 512], F32, 1, psum=True)
                PR = Ring(nc, sp2, "ps_p", [128, 2, 512], F32, 2, psum=True)
                OA = Ring(nc, sp2, "ps_oa", [128, 512], F32, 1, psum=True)
                dmy = pst(sp2, "ps_dmy", [128, 512], F32)
                EF = Ring(nc, sp2, "ef", [128, 2, 512], F32, 2)
                SPR = Ring(nc, sp2, "spb", [128, 2, 512], BF16, 5)
                AR = Ring(nc, sp2, "Ab", [128, 2, 512], BF16, 3)
                SRR = Ring(nc, sp2, "srun", [128, 2, 512], BF16, 4)

                def sb_att_job(ctx, p, j, u, nun, nst, kb, diag):
                    blk = nst * 4 + kb
                    c0 = kb * 128 if diag else 0
                    q0 = j * SBT + c0
                    q1 = (j + 1) * SBT
                    if u == 0:
                        ctx["oacc"] = OA.next()
                        ctx["sr"] = [SRR.next(), SRR.next()]
                        for (t_, k_) in ctx["sr"]:
                            S.op("pool", MS(t_[:], 0.0), writes=[k_])
                    kq_keys = [("KTs", nst, p), ("QTs", j, p), "cst"]
                    kts = [KTs[e * 64:e * 64 + 64, p, blk * 128:(blk + 1) * 128] for e in range(2)]
                    qts = [QTs[e * 64:e * 64 + 64, p, q0:q1] for e in range(2)]
                    zp, zk = ZR.next()
                    fns = []
                    for e in range(2):
                        fns.append(MM(zp[:, e, c0:], kts[e], qts[e], start=True, stop=not diag))
                        if diag:
                            fns.append(MM(zp[:, e, c0:c0 + 128], ident, trimask, start=False, stop=True))
                    S.group("pe", fns, reads=kq_keys, writes=[zk])
                    for _ in range(N_DUMMY):
                        S.op("pe", MM(dmy[:], ident, Vs[:, 0, :]), reads=["cst"], writes=[], signal=False)
                    ef, efk = EF.next()
                    sp, spk = SPR.next()
                    S.op("act", ACT(ef[:, :, c0:], zp[:, :, c0:], AF.Exp), reads=[zk], writes=[efk])
                    S.op("act", ACT(sp[:, :, c0:], ef[:, :, c0:], AF.Ln, bias=1.0), reads=[efk], writes=[spk])
                    yield
                    yield
                    (so, sok), (sn, snk) = ctx["sr"][u % 2], ctx["sr"][1 - u % 2]
                    Pp, Pk = PR.next()
                    fns = []
                    for e in range(2):
                        fns.append(MM(Pp[:, e, c0:], kts[e], qts[e], start=True, stop=False))
                        if diag:
                            fns.append(MM(Pp[:, e, c0:c0 + 128], ident, trimask, start=False, stop=False))
                        fns.append(MM(Pp[:, e, c0:], negtri, sp[:, e, c0:], start=False, stop=False))
                        fns.append(MM(Pp[:, e, c0:], negones, so[:, e, c0:], start=False, stop=True))
                    S.group("pe", fns, reads=kq_keys + [spk, sok], writes=[Pk])
                    A, Ak = AR.next()
                    S.op("act", ACT(A[:, :, c0:], Pp[:, :, c0:], AF.Exp), reads=[Pk], writes=[Ak])
                    S.op("dve", TT(sn[:, :, c0:], so[:, :, c0:], sp[:, :, c0:], ALU.add), reads=[sok, spk], writes=[snk])

                    def emit_av(ctx=ctx, p=p, j=j, u=u, nun=nun, blk=blk, c0=c0, A=A, Ak=Ak):
                        oacc, oak = ctx["oacc"]
                        S.group("pe", [MM(oacc[e * 64:e * 64 + 64, c0:], Vs[:, blk, (2 * p + e) * 64:(2 * p + e + 1) * 64], A[:, e, c0:],
                                          start=(u == 0), stop=(u == nun - 1)) for e in range(2)],
                                reads=[("Vs", blk), Ak], writes=[oak])
                        if u == nun - 1:
                            S.op("dve", CP(oaT[:, p, j * SBT:(j + 1) * SBT], oacc[:]), reads=[oak], writes=[("oaT", p, j)])

                    pend = pending_av[0]
                    pending_av[0] = emit_av
                    if pend is not None:
                        pend()

                pending_av = [None]
                jobs = []
                for p in range(4):
                    for j in range(4):
                        units = att_units(j)
                        ctx = {}
                        for u, (nst, kb, diag, _) in enumerate(units):
                            jobs.append(sb_att_job(ctx, p, j, u, len(units), nst, kb, diag))
                run_pipeline(jobs)
                pending_av[0]()

        S.fence()
        if dbg == "sb":
            S.dma(dbg_t, oaT, reads=[("oaT", p, j) for p in range(4) for j in range(4)])
            S.finish()
            return nc

        with ExitStack() as smg:
            Wg = sbt(smg, "Wg", [128, 8, 2 * D], BF16)
            Wo = sbt(smg, "Wo", [128, 8, D], BF16)
            Wba = sbt(smg, "Wba", [128, 4, D], BF16)
            Wbb = sbt(smg, "Wbb", [128, 4, D], BF16)
            for g4 in range(4):
                if g4 == 1:
                    S.dma(Wba[:], w_ba[0].rearrange("(k p) n -> p k n", p=128), writes=["Wba"], q="pool")
                    S.dma(Wbb[:], w_bb[0].rearrange("(k p) n -> p k n", p=128), writes=["Wbb"], q="pool")
                for half in range(2):
                    c0_ = half * D + g4 * 256
                    S.dma(Wg[:, :, c0_:c0_ + 256], w_gate[0, :, c0_:c0_ + 256].rearrange("(k p) n -> p k n", p=128),
                          writes=[("Wg", half, g4)], q="pool")
            S.dma(Wo[:], w_out[0].rearrange("(k p) n -> p k n", p=128), writes=["Wo"], q="pool")
            PG = Ring(nc, smg, "ps_g", [128, 512], F32, 6, psum=True)
            PY = Ring(nc, smg, "ps_y", [128, 512], F32, 2, psum=True)
            GS = Ring(nc, smg, "gs", [128, 512], BF16, 6)
            T12 = Ring(nc, smg, "t12", [128, 512], F32, 4)
            MT = Ring(nc, smg, "mT", [128, 8, SBT], BF16, 2)
            XR = Ring(nc, smg, "xr", [128, D], F32, 2)
            X1 = Ring(nc, smg, "x1r", [128, D], F32, 2)
            mts = {}

            def merge_ec_job(n, ec):
                if ec == 0:
                    mts[n] = MT.next()
                mT, mk = mts[n]
                hT, hk = hTown[n], "hTown%d" % n
                tok = slice(n * SBT, (n + 1) * SBT)
                g4 = ec // 2
                g0, g0k = PG.next()
                S.group("pe", [MM(g0[:], Wg[:, dc, ec * 128:(ec + 1) * 128], hT[:, dc, :], start=(dc == 0), stop=(dc == 7))
                               for dc in range(8)], reads=[hk, ("Wg", 0, g4)], writes=[g0k])
                gs0, gs0k = GS.next()
                S.op("act", ACT(gs0[:], g0[:], AF.Sigmoid, bias=bgate[:, ec:ec + 1]), reads=[g0k, "bgate"], writes=[gs0k])
                g1, g1k = PG.next()
                S.group("pe", [MM(g1[:], Wg[:, dc, D + ec * 128:D + (ec + 1) * 128], hT[:, dc, :], start=(dc == 0), stop=(dc == 7))
                               for dc in range(8)], reads=[hk, ("Wg", 1, g4)], writes=[g1k])
                gs1, gs1k = GS.next()
                S.op("act", ACT(gs1[:], g1[:], AF.Sigmoid, bias=bgate[:, 8 + ec:9 + ec]), reads=[g1k, "bgate"], writes=[gs1k])
                ba, bak = PG.next()
                S.group("pe", [MM(ba[:], Wba[:, cc, ec * 128:(ec + 1) * 128], oaT[:, cc, tok], start=(cc == 0), stop=(cc == 3))
                               for cc in range(4)], reads=["Wba"] + [("oaT", cc, n) for cc in range(4)], writes=[bak])
                bb, bbk = PG.next()
                S.group("pe", [MM(bb[:], Wbb[:, cc, ec * 128:(ec + 1) * 128], obT[:, cc, tok], start=(cc == 0), stop=(cc == 3))
                               for cc in range(4)], reads=["Wbb"] + [("obT", cc, n) for cc in range(4)], writes=[bbk])
                yield
                t1, t1k = T12.next()
                t2, t2k = T12.next()
                S.op("dve", TT(t1[:], ba[:], gs0[:], ALU.mult), reads=[bak, gs0k], writes=[t1k])
                S.op("dve", TT(t2[:], bb[:], gs1[:], ALU.mult), reads=[bbk, gs1k], writes=[t2k])
                S.op("pool", TT(mT[:, ec, :], t1[:], t2[:], ALU.add), reads=[t1k, t2k], writes=[(mk, ec)])

            def merge_y_job(n, t):
                mT, mk = mts[n]
                blk = n * 4 + t
                xt, xk = XR.next()
                S.dma(xt[:], xs[blk * 128:(blk + 1) * 128, :], writes=[xk])
                x1, x1k = X1.next()
                yield
                for hh in range(2):
                    y, yk = PY.next()
                    S.group("pe", [MM(y[:], mT[:, cc, t * 128:(t + 1) * 128], Wo[:, cc, hh * 512:(hh + 1) * 512],
                                      start=(cc == 0), stop=(cc == 7)) for cc in range(8)],
                            reads=[(mk, cc) for cc in range(8)] + ["Wo"], writes=[yk])
                    S.op("dve", TT(x1[:, hh * 512:(hh + 1) * 512], y[:], xt[:, hh * 512:(hh + 1) * 512], ALU.add),
                         reads=[yk, xk], writes=[x1k])
                S.dma(out[blk * 128:(blk + 1) * 128, :], x1[:], reads=[x1k], writes=[("out", blk)])

            jobs = []
            for n in range(4):
                jobs += [merge_ec_job(n, ec) for ec in range(8)]
                jobs += [merge_y_job(n, t) for t in range(4)]
            run_pipeline(jobs)

        s_att.close()
        S.fence()
        if dbg == "merge":
            S.finish()
            return nc

        with ExitStack() as sff:
            Wfg = sbt(sff, "Wfg", [128, 8, DFF], BF16)
            Wfu = sbt(sff, "Wfu", [128, 8, DFF], BF16)
            Wfd = sbt(sff, "Wfd", [128, NFC, D], BF16)
            FCG = [2, 4, 5, 5, 6]
            fc_lo = [sum(FCG[:i]) for i in range(len(FCG))]
            fc_slice = {}
            for k4, (lo_, n_) in enumerate(zip(fc_lo, FCG)):
                for fc_ in range(lo_, lo_ + n_):
                    fc_slice[fc_] = k4
                cs_ = slice(lo_ * 128, (lo_ + n_) * 128)
                S.dma(Wfg[:, :, cs_], w_fg[0, :, cs_].rearrange("(k p) n -> p k n", p=128), writes=[("Wfg", k4)], q="pool")
                S.dma(Wfu[:, :, cs_], w_fu[0, :, cs_].rearrange("(k p) n -> p k n", p=128), writes=[("Wfu", k4)], q="pool")
            S.dma(Wfd[:, 0:11, :], w_fd[0, 0:1408, :].rearrange("(k p) n -> p k n", p=128), writes=["Wfd0"], q="pool")
            S.dma(Wfd[:, 11:22, :], w_fd[0, 1408:2816, :].rearrange("(k p) n -> p k n", p=128), writes=["Wfd1"], q="pool")
            R = {
                "junk": Ring(nc, sff, "junk", [128, D], BF16, 1),
                "ss": Ring(nc, sff, "ss", [128, 4], F32, 3),
                "hb": Ring(nc, sff, "hb", [128, D], BF16, 2),
                "pT": Ring(nc, sff, "pT", [128, 8, 128], BF16, 2, psum=True),
            }
            H2 = Ring(nc, sff, "h2T", [128, 8, SBT], BF16, 2)
            XA = Ring(nc, sff, "x1a", [128, D], F32, 2)
            XB = Ring(nc, sff, "x1b", [128, D], F32, 3)
            FF = sbt(sff, "ffT", [128, NFC, SBT], BF16)
            SG = Ring(nc, sff, "sg", [128, 512], BF16, 3)
            PGU = Ring(nc, sff, "ps_gu", [128, 512], F32, 4, psum=True)
            PD = Ring(nc, sff, "ps_d", [128, 512], F32, 2, psum=True)
            h2s = {}

            def ffn_tile_job(n, t):
                if t == 0:
                    h2s[n] = H2.next()
                h2, h2k = h2s[n]
                blk = n * 4 + t
                xa, xak = XA.next()
                S.dma(xa[:], out[blk * 128:(blk + 1) * 128, :], reads=[("out", blk)], writes=[xak])
                hb, hbk = rms_stage(R, xa[:], xak, gffn_bc, "gffn_bc")
                yield
                tr_stage(R, hb, hbk, h2[:, :, t * 128:(t + 1) * 128], (h2k, t), blk)

            def ffn_fc_job(n, fc):
                h2, h2k = h2s[n]
                hkeys = [(h2k, t) for t in range(4)]
                wg_keys = [("Wfg", fc_slice[fc])]
                wu_keys = [("Wfu", fc_slice[fc])]
                pg, pgk = PGU.next()
                S.group("pe", [MM(pg[:], Wfg[:, dc, fc * 128:(fc + 1) * 128], h2[:, dc, :], start=(dc == 0), stop=(dc == 7))
                               for dc in range(8)], reads=hkeys + wg_keys, writes=[pgk])
                sg, sgk = SG.next()
                S.op("act", ACT(sg[:], pg[:], AF.Silu), reads=[pgk], writes=[sgk])
                pu, puk = PGU.next()
                S.group("pe", [MM(pu[:], Wfu[:, dc, fc * 128:(fc + 1) * 128], h2[:, dc, :], start=(dc == 0), stop=(dc == 7))
                               for dc in range(8)], reads=hkeys + wu_keys, writes=[puk])
                yield
                S.op("dve", TT(FF[:, fc, :], pu[:], sg[:], ALU.mult), reads=[puk, sgk], writes=[("ff", fc)])

            def ffn_down_job(n, t):
                blk = n * 4 + t
                xb, xbk = XB.next()
                S.dma(xb[:], out[blk * 128:(blk + 1) * 128, :], reads=[("out", blk)], writes=[xbk])
                yield
                for hh in range(2):
                    pd, pdk = PD.next()
                    S.group("pe", [MM(pd[:], FF[:, fc, t * 128:(t + 1) * 128], Wfd[:, fc, hh * 512:(hh + 1) * 512],
                                      start=(fc == 0), stop=(fc == NFC - 1)) for fc in range(NFC)],
                            reads=[("ff", fc) for fc in range(NFC)] + ["Wfd0", "Wfd1"], writes=[pdk])
                    S.op("dve", TT(xb[:, hh * 512:(hh + 1) * 512], pd[:], xb[:, hh * 512:(hh + 1) * 512], ALU.add),
                         reads=[pdk, xbk], writes=[xbk])
                S.dma(out[blk * 128:(blk + 1) * 128, :], xb[:], reads=[xbk], writes=[("out", blk)])

            jobs = [ffn_tile_job(0, t) for t in range(4)]
            for n in range(4):
                jobs += [ffn_fc_job(n, fc) for fc in range(NFC)]
                if n + 1 < 4:
                    jobs += [ffn_tile_job(n + 1, t) for t in range(4)]
                jobs += [ffn_down_job(n, t) for t in range(4)]
            run_pipeline(jobs)
        S.finish()
    return nc


def _host_consts():
    s = np.arange(128)[:, None]
    t = np.arange(128)[None, :]
    ident = (s == t).astype(np.float32)
    trimask = np.where(s >= t, NEG, 0.0).astype(np.float32)
    negtri = np.where(s >= t, -1.0, 0.0).astype(np.float32)
    negones = -np.ones((128, 128), np.float32)
    ones = np.ones((128, 128), np.float32)
    return np.concatenate([ident, trimask, negtri, negones, ones], axis=1)


def _core_inputs(inputs, b, c):
    x = np.asarray(inputs["x"], dtype=np.float32)
    xs = np.zeros((SEQ, D), np.float32)
    pos = np.zeros((SEQ,), np.float32)
    for i in range(4):
        o = 2 * i + c
        xs[i * SBT:(i + 1) * SBT] = x[b, o * SBT:(o + 1) * SBT]
        pos[i * SBT:(i + 1) * SBT] = np.arange(o * SBT, (o + 1) * SBT)
        xo = o - 1
        if xo >= 0:
            xs[(4 + i) * SBT:(5 + i) * SBT] = x[b, xo * SBT:(xo + 1) * SBT]
            pos[(4 + i) * SBT:(5 + i) * SBT] = np.arange(xo * SBT, (xo + 1) * SBT)
    inv_freq = (np.float32(500000.0) ** (-np.arange(0, 16, 2, dtype=np.float32) / np.float32(16))).astype(np.float32)
    ang = (pos[:, None] * inv_freq[None, :]).astype(np.float32)
    cs = np.concatenate([np.cos(ang), np.cos(ang), np.sin(ang), np.sin(ang)], axis=1).astype(np.float32)
    vones = np.full((128, 128), 1.0 if c == 1 else 0.0, np.float32).astype(ml_dtypes.bfloat16)
    cs = np.ascontiguousarray(cs.reshape(32, 128, 32).transpose(1, 0, 2).reshape(128, 32 * 32))
    bgl = np.ascontiguousarray(np.asarray(inputs["b_gate"], dtype=np.float32)[0].reshape(16, 128).T)
    f32 = lambda k: np.asarray(inputs[k], dtype=np.float32)[0]
    gqk = np.ascontiguousarray(np.broadcast_to(np.concatenate([np.tile(f32("g_q"), 8), np.tile(f32("g_k"), 8)])[None, :], (128, 1024)))
    lamb = np.ascontiguousarray(np.broadcast_to(np.concatenate([f32("lam_q1"), f32("lam_k1"), f32("lam_q2"), f32("lam_k2")])[None, :], (128, 256)))
    m = {"xs": xs, "cs": cs, "bgl": bgl, "gqk": gqk, "lamb": lamb, "vones": vones, "consts": _host_consts().astype(ml_dtypes.bfloat16)}
    for k in ("g_mix", "w_in", "g_q", "g_k", "lam_q1", "lam_k1", "lam_q2", "lam_k2", "g_sub", "w_branch_a",
              "w_branch_b", "w_gate", "b_gate", "w_out", "g_ffn", "w_ffn_gate", "w_ffn_up", "w_ffn_down"):
        m[k] = np.ascontiguousarray(np.asarray(inputs[k], dtype=np.float32))
    return m


def kernel(**inputs):
    nc = build_nc()
    in_maps = [_core_inputs(inputs, core // 2, core % 2) for core in range(8)]
    res = run_bass_kernel_spmd(nc, in_maps, core_ids=list(range(8)))
    outf = np.zeros((BATCH, SEQ, D), np.float32)
    for core in range(8):
        b, c = core // 2, core % 2
        o = np.asarray(res.results[core]["out"], dtype=np.float32)
        for i in range(4):
            sbi = 2 * i + c
            outf[b, sbi * SBT:(sbi + 1) * SBT] = o[i * SBT:(i + 1) * SBT]
    return outf
```
